# Optimizing a Trainium2 kernel written in Bass

```python
import jax, jax.numpy as jnp
from jax import lax
import numpy as np

D_MODEL = 2048
BATCH = 4
SEQ = 4096
DEPTH = 4
DEC_BATCH = 16
DEC_SEQ = 32
PAST_LEN = 4096

CHUNK = 64
N_A_LAYERS = DEPTH // 2
N_B_LAYERS = DEPTH - N_A_LAYERS
POOL_WINDOWS = (2, 4, 8, 16)
N_POOL_GROUPS = len(POOL_WINDOWS)
POOL_GROUP = D_MODEL // N_POOL_GROUPS
POOL_STATE = max(POOL_WINDOWS) - 1
HEAD_DIM = 64
N_HEADS = D_MODEL // HEAD_DIM
N_KV_HEADS = N_HEADS // 8
GQA_GROUP = N_HEADS // N_KV_HEADS
WINDOW = 128
WINDOW_CHUNKS = WINDOW // CHUNK
D_FF = 4 * D_MODEL
ROPE_THETA = 10000.0
EPS = 1e-6
ATTN_SCALE = HEAD_DIM ** -0.5
NEG_INF = -1e30

kernel_name = 'pool_swa_sink_yoco_stream'


def rmsnorm(x, g):
    xf = x.astype(jnp.float32)
    y = xf * lax.rsqrt(jnp.mean(xf * xf, axis=-1, keepdims=True) + EPS)
    return (y * g.astype(jnp.float32)).astype(x.dtype)


def modulate(x, shift, scale):
    return x * (1 + scale[:, None, :]) + shift[:, None, :]


def rope(x, pos):
    half = HEAD_DIM // 2
    inv = ROPE_THETA ** (-jnp.arange(half, dtype=jnp.float32) / half)
    ang = pos.astype(jnp.float32)[:, None] * inv[None, :]
    cos = jnp.cos(ang)[None, :, None, :]
    sin = jnp.sin(ang)[None, :, None, :]
    x1 = x[..., :half].astype(jnp.float32)
    x2 = x[..., half:].astype(jnp.float32)
    return jnp.concatenate([x1 * cos - x2 * sin, x2 * cos + x1 * sin], axis=-1).astype(x.dtype)


def pool_mix(h_ext, pos0, w_pool, pool_scale):
    B, L, D = h_ext.shape
    T = L - POOL_STATE
    hf = h_ext.astype(jnp.float32)
    cs = jnp.concatenate([jnp.zeros_like(hf[:, :1]), jnp.cumsum(hf, axis=1)], axis=1)
    end = cs[:, POOL_STATE + 1:]
    x_new = hf[:, POOL_STATE:]
    pos = pos0 + jnp.arange(T)
    outs = []
    for g, w in enumerate(POOL_WINDOWS):
        sl = slice(g * POOL_GROUP, (g + 1) * POOL_GROUP)
        start = cs[:, POOL_STATE + 1 - w:POOL_STATE + 1 - w + T, sl]
        cnt = jnp.minimum(w, pos + 1).astype(jnp.float32)[None, :, None]
        outs.append((end[..., sl] - start) / cnt - x_new[..., sl])
    pooled = jnp.stack(outs, axis=2).astype(h_ext.dtype)
    y = jnp.einsum('btgc,gcd->btgd', pooled, w_pool).reshape(B, T, D)
    return y * pool_scale


def sink_softmax(s, sink):
    sink = sink.astype(jnp.float32)
    m = jnp.maximum(jnp.max(s, axis=-1, keepdims=True), sink)
    e = jnp.exp(s - m)
    return e / (jnp.sum(e, axis=-1, keepdims=True) + jnp.exp(sink - m))


def attn_prompt(q, k, v, sink):
    B, S = q.shape[0], q.shape[1]
    n = S // CHUNK
    span = (WINDOW_CHUNKS + 1) * CHUNK
    pad = ((0, 0), (WINDOW_CHUNKS * CHUNK, 0), (0, 0), (0, 0))
    kc = jnp.pad(k, pad).reshape(B, n + WINDOW_CHUNKS, CHUNK, N_KV_HEADS, HEAD_DIM)
    vc = jnp.pad(v, pad).reshape(B, n + WINDOW_CHUNKS, CHUNK, N_KV_HEADS, HEAD_DIM)
    kb = jnp.concatenate([kc[:, i:i + n] for i in range(WINDOW_CHUNKS + 1)], axis=2)
    vb = jnp.concatenate([vc[:, i:i + n] for i in range(WINDOW_CHUNKS + 1)], axis=2)
    kpos = (jnp.arange(n)[:, None] - WINDOW_CHUNKS) * CHUNK + jnp.arange(span)[None, :]
    valid = kpos >= 0
    qb = q.reshape(B, n, CHUNK, N_KV_HEADS, GQA_GROUP, HEAD_DIM)
    s = jnp.einsum('bnqkgd,bnskd->bnkgqs', qb, kb, preferred_element_type=jnp.float32) * ATTN_SCALE
    s = jnp.where(valid[None, :, None, None, None, :], s, NEG_INF)
    p = sink_softmax(s, sink.reshape(N_KV_HEADS, GQA_GROUP)[None, None, :, :, None, None])
    o = jnp.einsum('bnkgqs,bnskd->bnqkgd', p.astype(v.dtype), vb)
    return o.reshape(B, S, N_HEADS * HEAD_DIM)


def attn_sample(q, k_new, v_new, k_past, v_past, sink):
    B, T = q.shape[0], q.shape[1]
    k = jnp.concatenate([k_past, k_new], axis=1)
    v = jnp.concatenate([v_past, v_new], axis=1)
    qg = q.reshape(B, T, N_KV_HEADS, GQA_GROUP, HEAD_DIM)
    s = jnp.einsum('btkgd,bskd->bkgts', qg, k, preferred_element_type=jnp.float32) * ATTN_SCALE
    p = sink_softmax(s, sink.reshape(N_KV_HEADS, GQA_GROUP)[None, :, :, None, None])
    o = jnp.einsum('bkgts,bskd->btkgd', p.astype(v.dtype), v)
    return o.reshape(B, T, N_HEADS * HEAD_DIM)


def shared_kv(x, c, pos, w_kv_mod, b_kv_mod, g_kv, w_kv):
    B, T, _ = x.shape
    mod = jax.nn.silu(c) @ w_kv_mod + b_kv_mod
    shift, scale = jnp.split(mod, 2, axis=-1)
    h = modulate(rmsnorm(x, g_kv), shift, scale)
    kv = h @ w_kv
    k, v = jnp.split(kv, 2, axis=-1)
    k = rope(k.reshape(B, T, N_KV_HEADS, HEAD_DIM), pos)
    v = v.reshape(B, T, N_KV_HEADS, HEAD_DIM)
    return k, v


def run_trunk(x, c, pos0, pool_prefix, kv_past, w_mod, b_mod, g_norm, w_pool, pool_scale,
              w_kv_mod, b_kv_mod, g_kv, w_kv, w_q, sinks, w_o, w_up, w_down):
    B, T, _ = x.shape
    pos = pos0 + jnp.arange(T)
    sc = jax.nn.silu(c)
    new_pool = []
    k = v = None
    for l in range(DEPTH):
        mod = sc @ w_mod[l] + b_mod[l]
        sh1, sc1, gt1, sh2, sc2, gt2 = jnp.split(mod, 6, axis=-1)
        h = modulate(rmsnorm(x, g_norm[l, 0]), sh1, sc1)
        if l < N_A_LAYERS:
            h_ext = jnp.concatenate([pool_prefix[l], h], axis=1)
            new_pool.append(h_ext[:, -POOL_STATE:])
            mix = pool_mix(h_ext, pos0, w_pool[l], pool_scale[l])
        else:
            j = l - N_A_LAYERS
            q = rope((h @ w_q[j]).reshape(B, T, N_HEADS, HEAD_DIM), pos)
            if kv_past is None:
                o = attn_prompt(q, k, v, sinks[j])
            else:
                o = attn_sample(q, k, v, kv_past[0], kv_past[1], sinks[j])
            mix = o @ w_o[j]
        x = x + gt1[:, None, :] * rmsnorm(mix, g_norm[l, 1])
        h = modulate(rmsnorm(x, g_norm[l, 2]), sh2, sc2)
        f = jnp.square(jax.nn.relu(h @ w_up[l])) @ w_down[l]
        x = x + gt2[:, None, :] * rmsnorm(f, g_norm[l, 3])
        if l == N_A_LAYERS - 1:
            k, v = shared_kv(x, c, pos, w_kv_mod, b_kv_mod, g_kv, w_kv)
    return x, jnp.stack(new_pool, axis=0), k, v


def setup_inputs(seed: int = 0) -> dict:
    key = jax.random.key(seed)
    ks = jax.random.split(key, 24)
    f32 = jnp.float32
    nrm = lambda k, shape, s: jax.random.normal(k, shape, f32) * s
    D = D_MODEL
    kv_rows = min(WINDOW, PAST_LEN)
    return {
        'x_prompt': nrm(ks[0], (BATCH, SEQ, D), 1.0),
        'x_sample': nrm(ks[1], (DEC_BATCH, DEC_SEQ, D), 1.0),
        'c_prompt': nrm(ks[2], (BATCH, D), 1.0),
        'c_sample': nrm(ks[3], (DEC_BATCH, D), 1.0),
        'state_pool': nrm(ks[4], (N_A_LAYERS, DEC_BATCH, POOL_STATE, D), 1.0),
        'cache_k': nrm(ks[5], (DEC_BATCH, kv_rows, N_KV_HEADS, HEAD_DIM), 1.0),
        'cache_v': nrm(ks[6], (DEC_BATCH, kv_rows, N_KV_HEADS, HEAD_DIM), 1.0),
        'w_mod': nrm(ks[7], (DEPTH, D, 6 * D), D ** -0.5),
        'b_mod': nrm(ks[8], (DEPTH, 6 * D), 0.02),
        'g_norm': 1.0 + nrm(ks[9], (DEPTH, 4, D), 0.02),
        'w_pool': nrm(ks[10], (N_A_LAYERS, N_POOL_GROUPS, POOL_GROUP, POOL_GROUP), POOL_GROUP ** -0.5),
        'pool_scale': 1.0 + nrm(ks[11], (N_A_LAYERS, D), 0.1),
        'w_kv_mod': nrm(ks[12], (D, 2 * D), D ** -0.5),
        'b_kv_mod': nrm(ks[13], (2 * D,), 0.02),
        'g_kv': 1.0 + nrm(ks[14], (D,), 0.02),
        'w_kv': nrm(ks[15], (D, 2 * N_KV_HEADS * HEAD_DIM), D ** -0.5),
        'w_q': nrm(ks[16], (N_B_LAYERS, D, N_HEADS * HEAD_DIM), D ** -0.5),
        'sinks': nrm(ks[17], (N_B_LAYERS, N_HEADS), 0.5),
        'w_o': nrm(ks[18], (N_B_LAYERS, N_HEADS * HEAD_DIM, D), (N_HEADS * HEAD_DIM) ** -0.5),
        'w_up': nrm(ks[19], (DEPTH, D, D_FF), D ** -0.5),
        'w_down': nrm(ks[20], (DEPTH, D_FF, D), D_FF ** -0.5),
    }


def reference(x_prompt, x_sample, c_prompt, c_sample, state_pool, cache_k, cache_v,
              w_mod, b_mod, g_norm, w_pool, pool_scale, w_kv_mod, b_kv_mod, g_kv, w_kv,
              w_q, sinks, w_o, w_up, w_down):
    prompt_prefix = jnp.zeros((N_A_LAYERS, x_prompt.shape[0], POOL_STATE, D_MODEL), x_prompt.dtype)
    y_prompt, pool_p, k_p, v_p = run_trunk(
        x_prompt, c_prompt, 0, prompt_prefix, None, w_mod, b_mod, g_norm, w_pool, pool_scale,
        w_kv_mod, b_kv_mod, g_kv, w_kv, w_q, sinks, w_o, w_up, w_down)
    y_sample, pool_s, k_s, v_s = run_trunk(
        x_sample, c_sample, PAST_LEN, state_pool, (cache_k, cache_v), w_mod, b_mod, g_norm, w_pool,
        pool_scale, w_kv_mod, b_kv_mod, g_kv, w_kv, w_q, sinks, w_o, w_up, w_down)
    kv_keep = min(WINDOW, x_prompt.shape[1])
    new_k_prompt = k_p[:, -kv_keep:]
    new_v_prompt = v_p[:, -kv_keep:]
    return (y_prompt, y_sample, pool_p, pool_s, new_k_prompt, new_v_prompt, k_s, v_s)
```

```python
import bisect
import os
import numpy as np
import concourse.bass as bass
import concourse.mybir as mybir
from concourse.bass_utils import run_bass_kernel_spmd

F32 = mybir.dt.float32
BF16 = mybir.dt.bfloat16
U8 = mybir.dt.uint8
AF = mybir.ActivationFunctionType
ALU = mybir.AluOpType

D = 2048
NK = 16
DFF = 8192
TM = 512
HALO = 192
NSS = 32
TH = HALO + 2 * NSS
NMAIN = int(os.environ.get("KDBG_NMAIN", "4"))
NCOLS = TH + NMAIN * TM
SCW = 256
NSLOT = 4
EPS = 1e-6
SCALE = 64 ** -0.5
SAME_SYNC = True
STAGE = int(os.environ.get("KDBG_STAGE", "1000"))
SUB = int(os.environ.get("KDBG_SUB", "1000"))
ASUB = int(os.environ.get("KDBG_ASUB", "1000"))


def gate(n):
    return n <= STAGE
DEPTH = 4


class IMap:
    def __init__(self, size):
        self.starts = [0]
        self.segs = [[0, size, None, {}]]

    def _split(self, pos):
        i = bisect.bisect_right(self.starts, pos) - 1
        seg = self.segs[i]
        if seg[0] == pos or pos >= seg[1]:
            return
        new = [pos, seg[1], seg[2], dict(seg[3])]
        seg[1] = pos
        self.starts.insert(i + 1, pos)
        self.segs.insert(i + 1, new)

    def cover(self, a, b):
        self._split(a)
        self._split(b)
        i = bisect.bisect_left(self.starts, a)
        out = []
        while i < len(self.segs) and self.segs[i][0] < b:
            out.append(self.segs[i])
            i += 1
        return out


class Prog:
    ENGS = ("pe", "act", "dve", "pool", "sync")

    def __init__(self, nc):
        self.nc = nc
        self.sems = []
        self.cnt = []
        self.q = {e: [] for e in self.ENGS}
        self.known = {e: {} for e in self.ENGS}
        self.cur = {}
        self.own = {e: set() for e in self.ENGS}
        self.maps = {"sb": IMap(1 << 20), "ps": IMap(8 * 2048)}
        self.dram = {}

    def new_sem(self, name):
        h = self.nc.alloc_semaphore(name=name)
        self.sems.append(h)
        self.cnt.append(0)
        return len(self.sems) - 1

    def phase(self, tag):
        for e in ("pe", "act", "dve"):
            s = self.new_sem(f"{e}_{tag}")
            self.cur[e] = s
            self.own[e].add(s)

    def _map(self, sp):
        if sp in self.maps:
            return self.maps[sp]
        m = self.maps[sp] = IMap(1 << 40)
        return m

    def op(self, eng, fn, reads=(), writes=(), dsem=None):
        need = {}
        for (sp, a, b) in reads:
            for seg in self._map(sp).cover(a, b):
                w = seg[2]
                if w is not None:
                    if need.get(w[0], 0) < w[1]:
                        need[w[0]] = w[1]
        for (sp, a, b) in writes:
            for seg in self._map(sp).cover(a, b):
                w = seg[2]
                if w is not None:
                    if need.get(w[0], 0) < w[1]:
                        need[w[0]] = w[1]
                for s_, v_ in seg[3].items():
                    if need.get(s_, 0) < v_:
                        need[s_] = v_
        waits = []
        kn = self.known[eng]
        for s_, v_ in need.items():
            if s_ in self.own[eng] and (eng == "pe" or not SAME_SYNC):
                continue
            if kn.get(s_, 0) >= v_:
                continue
            kn[s_] = v_
            waits.append((s_, v_))
        if dsem is None:
            sem = self.cur[eng]
            inc = 1
        else:
            sem = dsem
            inc = 16
        self.cnt[sem] += inc
        val = self.cnt[sem]
        for (sp, a, b) in reads:
            for seg in self._map(sp).cover(a, b):
                if seg[3].get(sem, 0) < val:
                    seg[3][sem] = val
        for (sp, a, b) in writes:
            for seg in self._map(sp).cover(a, b):
                seg[2] = (sem, val)
                seg[3] = {}
        self.q[eng].append((waits, fn, sem, inc))
        return (sem, val)

    def emit(self, eng, e, final_waits=()):
        sems = self.sems
        for (waits, fn, sem, inc) in self.q[eng]:
            for (s_, v_) in waits:
                e.wait_ge(sems[s_], v_)
            ins = fn(e)
            ins.then_inc(sems[sem], inc)
        for (s_, v_) in final_waits:
            e.wait_ge(sems[s_], v_)


class Buf:
    def __init__(self, big, off, parts, free, dtype, esz):
        self.off = off
        self.free = list(free)
        self.esz = esz
        n = 1
        for f in free:
            n *= f
        self.nbytes = n * esz
        ap = big[0:parts, off:off + self.nbytes].bitcast(dtype)
        if len(free) == 2:
            ap = ap.rearrange("p (a b) -> p a b", b=free[1])
        elif len(free) == 3:
            ap = ap.rearrange("p (a b c) -> p a b c", b=free[1], c=free[2])
        elif len(free) == 4:
            ap = ap.rearrange("p (a b c d) -> p a b c d", b=free[1], c=free[2], d=free[3])
        self.ap = ap
        self.sub = self.nbytes // free[0]

    def r(self, i=None, n=1):
        if i is None:
            return ("sb", self.off, self.off + self.nbytes)
        return ("sb", self.off + i * self.sub, self.off + (i + n) * self.sub)


def _esz(dt):
    return 4 if dt == F32 else 2


def build_program():
    nc = bass.Bass("TRN2", target_bir_lowering=False)
    pg = Prog(nc)

    def din(name, shape, dt=F32):
        return nc.dram_tensor(name, list(shape), dt, kind="ExternalInput").ap()

    def dout(name, shape, dt=F32):
        return nc.dram_tensor(name, list(shape), dt, kind="ExternalOutput").ap()

    xT = din("xT", [D, NCOLS])
    cT = din("cT", [128, NK, 3])
    gT = din("gT", [128, 16, NK])
    psT = din("psT", [128, 2, NK])
    bmodT = din("bmodT", [128, 4, 96])
    bkvT = din("bkvT", [128, 32])
    gkvT = din("gkvT", [128, NK])
    cosT = din("cosT", [128, NCOLS])
    sinT = din("sinT", [128, NCOLS])
    invc = din("invc", [128, 4, 16])
    hv = din("hv", [128, 1])
    pm = din("pm", [128, 128])
    sk = din("sk", [1, 64])
    spT = din("spT", [128, 2, NK, 2, 15])
    ckT = din("ckT", [128, 2, 4, 128])
    cvT = din("cvT", [64, 2, 2, 4, 128])
    w_mod = din("w_mod", [4, D, 6 * D])
    w_kv_mod = din("w_kv_mod", [D, 2 * D])
    w_pool = din("w_pool", [2, D, 512])
    wkd = din("wkd", [D, 512])
    wv = din("wv", [D, 256])
    w_q = din("w_q", [2, D, D])
    w_o = din("w_o", [2, D, D])
    w_up = din("w_up", [4, D, DFF])
    w_down = din("w_down", [4, DFF, D])

    yT = dout("yT", [D, 2 * NSS + NMAIN * TM])
    poolT = dout("poolT", [2, D, 45])
    kout = dout("kout", [4, 64, 192])
    vout = dout("vout", [4, 64, 256])

    TOTAL = 207 * 1024
    big = nc.alloc_sbuf_tensor("big", [128, TOTAL], U8)
    cursor = [0]

    def alloc(free, dt, parts=128, at=None):
        esz = _esz(dt)
        n = esz
        for f in free:
            n *= f
        if at is None:
            off = cursor[0]
            cursor[0] = (off + n + 31) // 32 * 32
        else:
            off = at
        return Buf(big, off, parts, free, dt, esz)

    X = alloc([NK, TM], F32)
    HB = alloc([NK, TM], BF16)
    A = alloc([32, TM], BF16)
    MIX = alloc([NK, TM], F32)
    SLOT = [alloc([NK, SCW], BF16) for _ in range(NSLOT)]
    QB = alloc([NK, TM], BF16, at=A.off)
    OB = alloc([NK, TM], BF16, at=A.off + 16384)
    EXTW = TM + 48
    HF = alloc([1, EXTW], F32, at=A.off)
    S0 = alloc([1, EXTW], F32, at=A.off + 4096)
    S1 = alloc([1, EXTW], F32, at=A.off + 8192)
    TMP2 = alloc([1, 16], F32, at=A.off + 12288)
    MODT = alloc([4, 96, 3], F32, at=A.off)
    MODK = alloc([32, 3], F32, at=A.off + 8192)
    G = alloc([16, NK], F32, at=A.off + 10240)
    BMOD = alloc([4, 96], F32, at=A.off + 12288)
    BKV = alloc([1, 32], F32, at=A.off + 14336)
    GKV = alloc([1, NK], F32, at=A.off + 15360)
    CT = alloc([NK, 3], F32, at=A.off + 16384)
    SKr = alloc([1, 64], F32, parts=1, at=A.off + 17408)
    QF = [alloc([1, TM], F32, at=MIX.off + i * 2048) for i in range(2)]
    T1 = [alloc([1, TM], F32, at=MIX.off + 4096 + i * 2048) for i in range(2)]
    T2 = [alloc([1, TM], F32, at=MIX.off + 8192 + i * 2048) for i in range(2)]
    PT = [alloc([3, TM], BF16, at=MIX.off + 12288 + i * 3072) for i in range(2)]
    RD = [alloc([1, TM], F32, at=MIX.off + 18432 + i * 2048) for i in range(2)]
    QH = [alloc([1, TM], BF16, at=MIX.off + 22528 + i * 1024) for i in range(2)]
    QL = [alloc([1, TM], BF16, at=MIX.off + 24576 + i * 1024) for i in range(2)]
    KTE = alloc([4, 128 + TM], BF16)
    KTO = alloc([4, 128 + TM], BF16)
    VB = alloc([10, 4, 128], BF16)
    CKTE = alloc([2, 4, 128], BF16)
    CKTO = alloc([2, 4, 128], BF16)
    CVB = alloc([2, 2, 4, 128], BF16)
    VONE = alloc([1, 128], BF16)
    VONE32 = alloc([1, 128], BF16)
    VHV = alloc([1, 128], BF16)
    E0 = alloc([1, 128], BF16)
    SQ = [alloc([1, TM], BF16) for _ in range(2)]
    TMP = [alloc([1, TM], F32, at=A.off + 28672 + i * 2048) for i in range(2)]
    ACC = alloc([1, TM], F32, at=A.off + 20480)
    SQF = [alloc([1, TM], F32, at=A.off + 22528 + i * 2048) for i in range(2)]
    ACH = alloc([1, TM], BF16, at=A.off + 26624)
    ACL = alloc([1, TM], BF16, at=A.off + 27648)
    RL = [alloc([1, TM], F32) for _ in range(1)]
    RSTD = alloc([1, TM], F32)
    COS = alloc([1, TM], F32)
    SIN = alloc([1, TM], F32)
    NPAR = 78
    PARAM = alloc([NPAR, NK], F32)
    PSC = alloc([2, NK], F32)
    PF = alloc([2, NK, 15], F32)
    SPF = alloc([2, NK, 2, 15], F32)
    KOUT = alloc([4, 192], F32, parts=64)
    VOUT = alloc([4, 256], F32, parts=64)
    PM = alloc([1, 128], F32)
    ONESD = alloc([1, 128], BF16)
    ONESF = alloc([1, 128], F32, parts=1)
    SKE = alloc([1, 64], F32, parts=1)
    INVC = alloc([4, 16], F32)
    HV = alloc([1, 1], F32)
    EPSB = alloc([1, 1], F32)
    SC3 = alloc([NK, 3], BF16)
    PMB = alloc([1, 128], BF16)
    SKH = alloc([1, 64], BF16)
    SKL = alloc([1, 64], BF16)
    SKT = alloc([1, 64], F32, parts=1)
    assert cursor[0] <= TOTAL, cursor[0]

    PS = [nc.alloc_psum_tensor(f"ps{i}", [128, 512], F32) for i in range(8)]

    def psr(b):
        return ("ps", b * 2048, (b + 1) * 2048)

    def par_idx(l, kind, j):
        return (l * 4 + kind) * 3 + j

    def ak_idx(j):
        return 48 + j

    def b_idx(l, which, j):
        return 51 + (l * 2 + which) * 3 + j

    def bk_idx(j):
        return 51 + 24 + j

    pg.phase("setup")
    s_setup = pg.new_sem("setup")
    s_setup2 = pg.new_sem("setup2")
    s_slot = [pg.new_sem(f"slot{i}") for i in range(NSLOT)]
    s_x = pg.new_sem("xload")
    s_tab = pg.new_sem("tab")
    s_y = pg.new_sem("ystore")
    s_out = pg.new_sem("outs")

    slab_ctr = [0]

    def load_slab(src2d, ncols=SCW):
        i = slab_ctr[0] % NSLOT
        slab_ctr[0] += 1
        slot = SLOT[i]
        src = src2d.rearrange("(k p) n -> p k n", p=128)
        dst = slot.ap[:, :, 0:ncols]
        pg.op("pool", lambda e, d=dst, s=src: e.dma_start(out=d, in_=s), reads=[], writes=[slot.r()], dsem=s_slot[i])
        return slot

    bank_ctr = [0]

    def next_bank():
        b = bank_ctr[0] % 4
        bank_ctr[0] += 1
        return b

    def mm_group(lst, reads, bank):
        def fn(e, lst=lst):
            last = None
            for (o, l, r, st, sp) in lst:
                last = e.matmul(o, l, r, start=st, stop=sp)
            return last
        pg.op("pe", fn, reads=reads, writes=[psr(bank)])

    setup_loads = []

    def sload(eng, dst_buf, dst_ap, src_ap):
        pg.op(eng, lambda e, d=dst_ap, s=src_ap: e.dma_start(out=d, in_=s), reads=[], writes=[dst_buf.r()],
              dsem=(s_setup if eng == "sync" else s_setup2))

    sload("sync", CT, CT.ap, cT)
    sload("sync", G, G.ap, gT)
    sload("sync", PSC, PSC.ap, psT)
    sload("sync", BMOD, BMOD.ap, bmodT)
    sload("sync", BKV, BKV.ap[:, 0, :], bkvT)
    sload("sync", GKV, GKV.ap[:, 0, :], gkvT)
    sload("sync", INVC, INVC.ap, invc)
    sload("sync", HV, HV.ap[:, 0, :], hv)
    sload("sync", PM, PM.ap[:, 0, :], pm)
    sload("sync", SKr, SKr.ap[:, 0, :], sk)
    sload("sync", SPF, SPF.ap, spT)
    tot = pg.cnt[s_setup]
    for b in (CT, G, PSC, BMOD, BKV, GKV, INVC, HV, PM, SKr, SPF):
        for seg in pg.maps["sb"].cover(b.off, b.off + b.nbytes):
            seg[2] = (s_setup, tot)

    def dve(fn, reads, writes):
        pg.op("dve", fn, reads=reads, writes=writes)

    def act(fn, reads, writes):
        pg.op("act", fn, reads=reads, writes=writes)

    dve(lambda e: e.memset(ONESD.ap[:, 0, :], 1.0 / D), [], [ONESD.r()])
    dve(lambda e: e.memset(ONESF.ap[:, 0, :], 1.0), [], [ONESF.r()])
    for cbuf in (VONE, VONE32, VHV, E0, SKH, SKL, CKTE, CKTO, CVB, KTO):
        dve(lambda e, c=cbuf: e.memset(c.ap, 0.0), [], [cbuf.r()])
    dve(lambda e: e.memset(VONE.ap[0:64, 0, :], 1.0), [], [VONE.r()])
    dve(lambda e: e.memset(VONE32.ap[0:32, 0, :], 1.0), [], [VONE32.r()])
    dve(lambda e: e.memset(E0.ap[0:1, 0, :], 1.0), [], [E0.r()])
    pg.op("pool", lambda e: e.dma_start(out=CKTE.ap[0:64], in_=ckT[0:64]), reads=[], writes=[CKTE.r()], dsem=s_setup2)
    pg.op("pool", lambda e: e.dma_start(out=CKTO.ap[64:128], in_=ckT[64:128]), reads=[], writes=[CKTO.r()], dsem=s_setup2)
    pg.op("pool", lambda e: e.dma_start(out=CVB.ap[0:64], in_=cvT), reads=[], writes=[CVB.r()], dsem=s_setup2)
    tot2 = pg.cnt[s_setup2]
    for b in (CKTE, CKTO, CVB):
        for seg in pg.maps["sb"].cover(b.off, b.off + b.nbytes):
            seg[2] = (s_setup2, tot2)
    dve(lambda e: e.memset(PF.ap, 0.0), [], [PF.r()])
    dve(lambda e: e.memset(EPSB.ap[:, 0, :], EPS), [], [EPSB.r()])
    dve(lambda e: e.memset(VOUT.ap, 0.0), [], [VOUT.r()])
    dve(lambda e: e.memset(KOUT.ap, 0.0), [], [KOUT.r()])
    dve(lambda e: e.memset(VB.ap, 0.0), [], [VB.r()])
    dve(lambda e: e.memset(KTE.ap, 0.0), [], [KTE.r()])
    dve(lambda e: e.tensor_scalar(VHV.ap[:, 0, :], VONE.ap[:, 0, :], HV.ap[:, 0, :], None, ALU.mult),
        [VONE.r(), HV.r()], [VHV.r()])
    act(lambda e: e.activation(out=SC3.ap, in_=CT.ap, func=AF.Silu), [CT.r()], [SC3.r()])
    act(lambda e: e.activation(out=SKE.ap[:, 0, :], in_=SKr.ap[:, 0, :], func=AF.Exp), [SKr.r()], [SKE.r()])
    dve(lambda e: e.tensor_copy(PMB.ap[:, 0, :], PM.ap[:, 0, :]), [PM.r()], [PMB.r()])
    dve(lambda e: e.tensor_copy(SKH.ap[0:1, 0, :], SKE.ap[:, 0, :]), [SKE.r()], [SKH.r()])
    dve(lambda e: e.tensor_tensor(SKT.ap[:, 0, :], SKE.ap[:, 0, :], SKH.ap[0:1, 0, :], ALU.subtract), [SKE.r(), SKH.r()], [SKT.r()])
    dve(lambda e: e.tensor_copy(SKL.ap[0:1, 0, :], SKT.ap[:, 0, :]), [SKT.r()], [SKL.r()])

    def mod_pass(wsrc, ncol, dst_fn, bias_fn):
        for s in range(ncol // SCW):
            slot = load_slab(wsrc[:, s * SCW:(s + 1) * SCW])
            bank = 4 + (s % 2)
            lst = []
            for mm in range(2):
                for k in range(NK):
                    lst.append((PS[bank][:, mm * 3:(mm + 1) * 3], slot.ap[:, k, mm * 128:(mm + 1) * 128],
                                SC3.ap[:, k, :], k == 0, k == NK - 1))
            mm_group(lst, [slot.r(), SC3.r()], bank)
            psv = PS[bank][:, 0:6].rearrange("p (a b) -> p a b", b=3)
            for j in range(3):
                d_ap, d_r = dst_fn(s, j)
                dve(lambda e, o=d_ap, i0=psv[:, :, j], i1=bias_fn(s): e.tensor_tensor(o, i0, i1, ALU.add),
                    [psr(bank), BMOD.r(), BKV.r()], [d_r])

    for l in range(DEPTH if gate(-1) else 0):
        mod_pass(w_mod[l], 6 * D,
                 lambda s, j, l=l: (MODT.ap[:, l, 2 * s:2 * s + 2, j], MODT.r(l)),
                 lambda s, l=l: BMOD.ap[:, l, 2 * s:2 * s + 2])
    if gate(-1):
        mod_pass(w_kv_mod, 2 * D,
                 lambda s, j: (MODK.ap[:, 2 * s:2 * s + 2, j], MODK.r()),
                 lambda s: BKV.ap[:, 0, 2 * s:2 * s + 2])

    for l in range(DEPTH):
        for j in range(3):
            for (kind, sc_off, gi, plus1) in ((0, 16, 0, True), (1, 32, 1, False), (2, 64, 2, True), (3, 80, 3, False)):
                o = PARAM.ap[:, par_idx(l, kind, j), :]
                i0 = MODT.ap[:, l, sc_off:sc_off + 16, j]
                i1 = G.ap[:, l * 4 + gi, :]
                if plus1:
                    dve(lambda e, o=o, i0=i0, i1=i1: e.scalar_tensor_tensor(o, i0, 1.0, i1, ALU.add, ALU.mult),
                        [MODT.r(l), G.r()], [PARAM.r(par_idx(l, kind, j))])
                else:
                    dve(lambda e, o=o, i0=i0, i1=i1: e.tensor_tensor(o, i0, i1, ALU.mult),
                        [MODT.r(l), G.r()], [PARAM.r(par_idx(l, kind, j))])
            for which, off in ((0, 0), (1, 48)):
                o = PARAM.ap[:, b_idx(l, which, j), :]
                i0 = MODT.ap[:, l, off:off + 16, j]
                dve(lambda e, o=o, i0=i0: e.tensor_copy(o, i0), [MODT.r(l)], [PARAM.r(b_idx(l, which, j))])
    for j in range(3):
        o = PARAM.ap[:, ak_idx(j), :]
        dve(lambda e, o=o, i0=MODK.ap[:, 16:32, j], i1=GKV.ap[:, 0, :]: e.scalar_tensor_tensor(o, i0, 1.0, i1, ALU.add, ALU.mult),
            [MODK.r(), GKV.r()], [PARAM.r(ak_idx(j))])
        o2 = PARAM.ap[:, bk_idx(j), :]
        dve(lambda e, o=o2, i0=MODK.ap[:, 0:16, j]: e.tensor_copy(o, i0), [MODK.r()], [PARAM.r(bk_idx(j))])

    def mean_sq(src, Tt):
        bank = 4
        act(lambda e, o=ACC.ap[:, 0, 0:Tt], i=src.ap[:, 0, 0:Tt]: e.activation(out=o, in_=i, func=AF.Square),
            [src.r(0)], [ACC.r()])
        for k in range(1, NK):
            sq = SQF[k % 2]
            act(lambda e, o=sq.ap[:, 0, 0:Tt], i=src.ap[:, k, 0:Tt]: e.activation(out=o, in_=i, func=AF.Square),
                [src.r(k)], [sq.r()])
            dve(lambda e, o=ACC.ap[:, 0, 0:Tt], i1=sq.ap[:, 0, 0:Tt]: e.tensor_tensor(o, o, i1, ALU.add),
                [ACC.r(), sq.r()], [ACC.r()])
        act(lambda e: e.activation(out=ACH.ap[:, 0, 0:Tt], in_=ACC.ap[:, 0, 0:Tt], func=AF.Copy), [ACC.r()], [ACH.r()])
        dve(lambda e: e.tensor_tensor(ACL.ap[:, 0, 0:Tt], ACC.ap[:, 0, 0:Tt], ACH.ap[:, 0, 0:Tt], ALU.subtract),
            [ACC.r(), ACH.r()], [ACL.r()])
        mm_group([(PS[bank][:, 0:Tt], ONESD.ap[:, 0, :], ACH.ap[:, 0, 0:Tt], True, False),
                  (PS[bank][:, 0:Tt], ONESD.ap[:, 0, :], ACL.ap[:, 0, 0:Tt], False, True)], [ACH.r(), ACL.r(), ONESD.r()], bank)
        act(lambda e: e.activation(out=RSTD.ap[:, 0, 0:Tt], in_=PS[bank][:, 0:Tt], func=AF.Ln, bias=EPSB.ap[:, 0, :], scale=1.0),
            [psr(bank), EPSB.r()], [RSTD.r()])
        act(lambda e: e.activation(out=RSTD.ap[:, 0, 0:Tt], in_=RSTD.ap[:, 0, 0:Tt], func=AF.Exp, scale=-0.5),
            [RSTD.r()], [RSTD.r()])

    def prenorm(Tt, segs, a_idx, b_idx_fn, dest_fn, per_chunk_post=None):
        mean_sq(X, Tt)
        for k in range(NK):
            tmp = TMP[k % 2]
            for si, seg in enumerate(segs):
                c0, n, j = seg["c0"], seg["n"], seg["j"]
                a_ap = PARAM.ap[:, a_idx(j), k:k + 1]
                dve(lambda e, o=tmp.ap[:, 0, c0:c0 + n], i0=X.ap[:, k, c0:c0 + n], s=a_ap, i1=RSTD.ap[:, 0, c0:c0 + n]:
                    e.scalar_tensor_tensor(o, i0, s, i1, ALU.mult, ALU.mult),
                    [X.r(k), PARAM.r(a_idx(j)), RSTD.r()], [tmp.r()])
            for si, seg in enumerate(segs):
                c0, n, j = seg["c0"], seg["n"], seg["j"]
                d_ap, d_r = dest_fn(k, si, seg)
                bi = b_idx_fn(j)
                act(lambda e, o=d_ap, i=tmp.ap[:, 0, c0:c0 + n], b=PARAM.ap[:, bi, k:k + 1]:
                    e.activation(out=o, in_=i, func=AF.Identity, bias=b, scale=1.0),
                    [tmp.r(), PARAM.r(bi)], [d_r])
            if per_chunk_post is not None:
                per_chunk_post(k)

    def hb_dest(k, si, seg):
        return HB.ap[:, k, seg["c0"]:seg["c0"] + seg["n"]], HB.r(k)

    def postnorm_update(Tt, segs, gg_idx):
        mean_sq(MIX, Tt)
        for k in range(NK):
            tmp = TMP[k % 2]
            for seg in segs:
                c0, n, j = seg["c0"], seg["n"], seg["j"]
                dve(lambda e, o=tmp.ap[:, 0, c0:c0 + n], i0=MIX.ap[:, k, c0:c0 + n], s=PARAM.ap[:, gg_idx(j), k:k + 1],
                    i1=RSTD.ap[:, 0, c0:c0 + n]: e.scalar_tensor_tensor(o, i0, s, i1, ALU.mult, ALU.mult),
                    [MIX.r(k), PARAM.r(gg_idx(j)), RSTD.r()], [tmp.r()])
            dve(lambda e, o=X.ap[:, k, 0:Tt], i0=X.ap[:, k, 0:Tt], i1=tmp.ap[:, 0, 0:Tt]: e.tensor_tensor(o, i0, i1, ALU.add),
                [X.r(k), tmp.r()], [X.r(k)])

    def pool_layer(l, Tt, segs, first_main, is_hs):
        eo = [seg["c0"] + 15 * si for si, seg in enumerate(segs)]

        def dest(k, si, seg):
            return HF.ap[:, 0, eo[si] + 15:eo[si] + 15 + seg["n"]], HF.r()

        def pre_chunk(k):
            for si, seg in enumerate(segs):
                if seg["pfx"] == "PF":
                    src, sr = PF.ap[:, l, k, :], PF.r(l)
                else:
                    src, sr = SPF.ap[:, l, k, seg["pfx"], :], SPF.r(l)
                dve(lambda e, o=HF.ap[:, 0, eo[si]:eo[si] + 15], i=src: e.tensor_copy(o, i), [sr], [HF.r()])

        def post_chunk(k):
            g = k // 4
            w = 2 << g
            for si, seg in enumerate(segs):
                n = seg["n"]
                cur = HF
                lo = eo[si]
                hi = eo[si] + 15 + n
                for step in range(g + 1):
                    sh = 1 << step
                    dst = S0 if step % 2 == 0 else S1
                    r0 = lo + (2 << step) - 1
                    dve(lambda e, o=dst.ap[:, 0, r0:hi], i0=cur.ap[:, 0, r0:hi], i1=cur.ap[:, 0, r0 - sh:hi - sh]:
                        e.tensor_tensor(o, i0, i1, ALU.add), [cur.r()], [dst.r()])
                    cur = dst
                c0 = seg["c0"]
                dve(lambda e, o=HB.ap[:, k, c0:c0 + n], i0=cur.ap[:, 0, lo + 15:hi], i1=HF.ap[:, 0, lo + 15:hi], w=w:
                    e.scalar_tensor_tensor(o, i0, 1.0 / w, i1, ALU.mult, ALU.subtract),
                    [cur.r(), HF.r()], [HB.r(k)])
                if first_main:
                    dve(lambda e, o=TMP2.ap[:, 0, :], i0=cur.ap[:, 0, lo + 15:lo + 31], i1=INVC.ap[:, g, :]:
                        e.tensor_tensor(o, i0, i1, ALU.mult), [cur.r(), INVC.r()], [TMP2.r()])
                    dve(lambda e, o=HB.ap[:, k, c0:c0 + 16], i0=TMP2.ap[:, 0, :], i1=HF.ap[:, 0, lo + 15:lo + 31]:
                        e.tensor_tensor(o, i0, i1, ALU.subtract), [TMP2.r(), HF.r()], [HB.r(k)])
                if seg["pfx"] == "PF":
                    if is_hs:
                        dve(lambda e, o=PF.ap[:, l, k, :], i=HF.ap[:, 0, hi - 15:hi]:
                            e.tensor_scalar(o, i, HV.ap[:, 0, :], None, ALU.mult), [HF.r(), HV.r()], [PF.r(l)])
                    else:
                        dve(lambda e, o=PF.ap[:, l, k, :], i=HF.ap[:, 0, hi - 15:hi]: e.tensor_copy(o, i), [HF.r()], [PF.r(l)])
                else:
                    dve(lambda e, o=SPF.ap[:, l, k, seg["pfx"], :], i=HF.ap[:, 0, hi - 15:hi]: e.tensor_copy(o, i),
                        [HF.r()], [SPF.r(l)])

        mean_sq(X, Tt)
        for k in range(NK):
            pre_chunk(k)
            tmp = TMP[k % 2]
            for si, seg in enumerate(segs):
                c0, n, j = seg["c0"], seg["n"], seg["j"]
                dve(lambda e, o=tmp.ap[:, 0, c0:c0 + n], i0=X.ap[:, k, c0:c0 + n], s=PARAM.ap[:, par_idx(l, 0, j), k:k + 1],
                    i1=RSTD.ap[:, 0, c0:c0 + n]: e.scalar_tensor_tensor(o, i0, s, i1, ALU.mult, ALU.mult),
                    [X.r(k), PARAM.r(par_idx(l, 0, j)), RSTD.r()], [tmp.r()])
            for si, seg in enumerate(segs):
                c0, n, j = seg["c0"], seg["n"], seg["j"]
                d_ap, d_r = dest(k, si, seg)
                bi = b_idx(l, 0, j)
                act(lambda e, o=d_ap, i=tmp.ap[:, 0, c0:c0 + n], b=PARAM.ap[:, bi, k:k + 1]:
                    e.activation(out=o, in_=i, func=AF.Identity, bias=b, scale=1.0),
                    [tmp.r(), PARAM.r(bi)], [d_r])
            post_chunk(k)
        for cb in range(2):
            slot = load_slab(w_pool[l][:, cb * SCW:(cb + 1) * SCW])
            for g in range(4):
                for mm in range(2):
                    m = g * 4 + cb * 2 + mm
                    bank = next_bank()
                    lst = [(PS[bank][:, 0:Tt], slot.ap[:, g * 4 + kk, mm * 128:(mm + 1) * 128], HB.ap[:, g * 4 + kk, 0:Tt],
                            kk == 0, kk == 3) for kk in range(4)]
                    mm_group(lst, [slot.r(), HB.r(g * 4, 4)], bank)
                    dve(lambda e, o=MIX.ap[:, m, 0:Tt], i=PS[bank][:, 0:Tt], s=PSC.ap[:, l, m:m + 1]:
                        e.tensor_scalar(o, i, s, None, ALU.mult), [psr(bank), PSC.r()], [MIX.r(m)])
        postnorm_update(Tt, segs, lambda j: par_idx(l, 1, j))

    def mlp_layer(l, Tt, segs):
        prenorm(Tt, segs, lambda j: par_idx(l, 2, j), lambda j: b_idx(l, 1, j), hb_dest)
        for hf in range(2):
            for s in range(16):
                slot = load_slab(w_up[l][:, (hf * 16 + s) * SCW:(hf * 16 + s + 1) * SCW])
                for mm in range(2):
                    bank = next_bank()
                    lst = [(PS[bank][:, 0:Tt], slot.ap[:, k, mm * 128:(mm + 1) * 128], HB.ap[:, k, 0:Tt], k == 0, k == NK - 1)
                           for k in range(NK)]
                    mm_group(lst, [slot.r(), HB.r()], bank)
                    rl = RL[0]
                    act(lambda e, o=rl.ap[:, 0, 0:Tt], i=PS[bank][:, 0:Tt]: e.activation(out=o, in_=i, func=AF.Relu),
                        [psr(bank)], [rl.r()])
                    dve(lambda e, o=A.ap[:, 2 * s + mm, 0:Tt], i=rl.ap[:, 0, 0:Tt]: e.tensor_tensor(o, i, i, ALU.mult),
                        [rl.r()], [A.r(2 * s + mm)])
            for cb in range(8):
                banks = [next_bank(), next_bank()]
                for kb in range(2):
                    r0 = (hf * 2 + kb) * D
                    slot = load_slab(w_down[l][r0:r0 + D, cb * SCW:(cb + 1) * SCW])
                    for mm in range(2):
                        lst = [(PS[banks[mm]][:, 0:Tt], slot.ap[:, k, mm * 128:(mm + 1) * 128], A.ap[:, kb * 16 + k, 0:Tt],
                                kb == 0 and k == 0, kb == 1 and k == NK - 1) for k in range(NK)]
                        mm_group(lst, [slot.r(), A.r(kb * 16, 16)], banks[mm])
                for mm in range(2):
                    m = 2 * cb + mm
                    if hf == 0:
                        act(lambda e, o=MIX.ap[:, m, 0:Tt], i=PS[banks[mm]][:, 0:Tt]: e.activation(out=o, in_=i, func=AF.Copy),
                            [psr(banks[mm])], [MIX.r(m)])
                    else:
                        dve(lambda e, o=MIX.ap[:, m, 0:Tt], i0=PS[banks[mm]][:, 0:Tt], i1=MIX.ap[:, m, 0:Tt]:
                            e.tensor_tensor(o, i0, i1, ALU.add), [psr(banks[mm]), MIX.r(m)], [MIX.r(m)])
        postnorm_update(Tt, segs, lambda j: par_idx(l, 3, j))

    blk_ctr = [0]

    def attn_block(j2, q0, nq, g, pieces):
        bi = blk_ctr[0]
        blk_ctr[0] += 1
        N = 8 * nq
        half = 4 * nq
        pt = PT[bi % 2]
        rd = RD[bi % 2]
        for p, (kte_ap, kto_ap, kt_r, v_ap, v_r, val_ap, val_r, nk) in enumerate(pieces):
            sb = (bi % 2) * 3 + p
            if sb >= 4:
                sb = sb
            lst = [(PS[sb][0:nk, 0:half].rearrange("p (j q) -> p j q", q=nq), kte_ap, QB.ap[:, 4 * g:4 * g + 4, q0:q0 + nq], True, True),
                   (PS[sb][0:nk, half:N].rearrange("p (j q) -> p j q", q=nq), kto_ap, QB.ap[:, 4 * g:4 * g + 4, q0:q0 + nq], True, True)]
            mm_group(lst, kt_r + [QB.r(4 * g, 4)], sb)
            act(lambda e, o=pt.ap[0:nk, p, 0:N], i=PS[sb][0:nk, 0:N]: e.activation(out=o, in_=i, func=AF.Exp, scale=SCALE),
                [psr(sb)], [pt.r(p)])
        np_ = len(pieces)
        if ASUB < 2:
            return
        lst = [(PS[6][:, 0:N], v_ap, pt.ap[:, p, 0:N], p == 0, p == np_ - 1)
               for p, (_, _, _, v_ap, v_r, _, _, nk) in enumerate(pieces)]
        mm_group(lst, [pt.r()] + [pc[4] for pc in pieces], 6)
        lst = [(PS[7][:, 0:N], val_ap, pt.ap[:, p, 0:N], p == 0, False)
               for p, (_, _, _, _, _, val_ap, val_r, nk) in enumerate(pieces)]
        o3 = PS[7][:, 0:N].rearrange("p (j q) -> p j q", q=nq)
        skh_ap = SKH.ap[:, 0, j2 * 32 + g * 8:j2 * 32 + g * 8 + 8].unsqueeze(2).to_broadcast([128, 8, nq])
        skl_ap = SKL.ap[:, 0, j2 * 32 + g * 8:j2 * 32 + g * 8 + 8].unsqueeze(2).to_broadcast([128, 8, nq])
        lst.append((o3, E0.ap[:, 0, :], skh_ap, False, False))
        lst.append((o3, E0.ap[:, 0, :], skl_ap, False, True))
        mm_group(lst, [pt.r(), VONE.r(), VONE32.r(), VHV.r(), E0.r(), SKH.r(), SKL.r()], 7)
        if ASUB < 3:
            return
        act(lambda e, o=rd.ap[:, 0, 0:N], i=PS[7][:, 0:N]: e.activation(out=o, in_=i, func=AF.Ln), [psr(7)], [rd.r()])
        act(lambda e, o=rd.ap[:, 0, 0:N]: e.activation(out=o, in_=o, func=AF.Exp, scale=-1.0), [rd.r()], [rd.r()])
        if ASUB < 4:
            return
        dve(lambda e, o=OB.ap[0:64, 4 * g:4 * g + 4, q0:q0 + nq],
            i0=PS[6][0:64, 0:half].rearrange("p (j q) -> p j q", q=nq),
            i1=rd.ap[0:64, 0, 0:half].rearrange("p (j q) -> p j q", q=nq): e.tensor_tensor(o, i0, i1, ALU.mult),
            [psr(6), rd.r()], [OB.r(4 * g, 4)])
        dve(lambda e, o=OB.ap[64:128, 4 * g:4 * g + 4, q0:q0 + nq],
            i0=PS[6][64:128, half:N].rearrange("p (j q) -> p j q", q=nq),
            i1=rd.ap[64:128, 0, half:N].rearrange("p (j q) -> p j q", q=nq): e.tensor_tensor(o, i0, i1, ALU.mult),
            [psr(6), rd.r()], [OB.r(4 * g, 4)])

    def attn_layer(l, Tt, segs, blocks):
        j2 = l - 2
        prenorm(Tt, segs, lambda j: par_idx(l, 0, j), lambda j: b_idx(l, 0, j), hb_dest)
        def q_fin_factory(m):
            def fin(t1, t2):
                dve(lambda e, o=QB.ap[:, m, 0:Tt], i0=t1.ap[:, 0, 0:Tt], i1=t2.ap[:, 0, 0:Tt]: e.tensor_tensor(o, i0, i1, ALU.add),
                    [t1.r(), t2.r()], [QB.r(m)])
            return fin

        def gen_q():
            cur_slot = None
            for m in range(NK):
                if m % 2 == 0:
                    cur_slot = load_slab(w_q[j2][:, (m // 2) * SCW:(m // 2 + 1) * SCW])
                mm = m % 2
                lst = [(None, cur_slot.ap[:, k, mm * 128:(mm + 1) * 128], HB.ap[:, k, 0:Tt], k == 0, k == NK - 1) for k in range(NK)]
                yield (lst, [cur_slot.r(), HB.r()], q_fin_factory(m))

        rope_chunks_gen(gen_q(), Tt)
        if Tt == TH:
            dve(lambda e: e.memset(OB.ap[:, :, 0:HALO], 0.0), [], [OB.r()])
        for pt_ in PT:
            dve(lambda e, o=pt_.ap[64:128, :, :]: e.memset(o, 0.0), [], [pt_.r()])
            if Tt == TH:
                dve(lambda e, o=pt_.ap[32:64, 2, :]: e.memset(o, 0.0), [], [pt_.r()])
        for (q0, nq, g, pieces) in blocks:
            if ASUB >= 1:
                attn_block(j2, q0, nq, g, pieces)
        if ASUB < 5:
            return
        for s in range(8):
            slot = load_slab(w_o[j2][:, s * SCW:(s + 1) * SCW])
            for mm in range(2):
                m = 2 * s + mm
                bank = next_bank()
                lst = [(PS[bank][:, 0:Tt], slot.ap[:, k, mm * 128:(mm + 1) * 128], OB.ap[:, k, 0:Tt], k == 0, k == NK - 1)
                       for k in range(NK)]
                mm_group(lst, [slot.r(), OB.r()], bank)
                act(lambda e, o=MIX.ap[:, m, 0:Tt], i=PS[bank][:, 0:Tt]: e.activation(out=o, in_=i, func=AF.Copy),
                    [psr(bank)], [MIX.r(m)])
        postnorm_update(Tt, segs, lambda j: par_idx(l, 1, j))

    def rope_chunks_gen(gen, Tt):
        pending = None

        def finish(p):
            qf, t1, t2, fin, qh, ql = p
            act(lambda e, o=qh.ap[:, 0, 0:Tt], i=qf.ap[:, 0, 0:Tt]: e.activation(out=o, in_=i, func=AF.Copy), [qf.r()], [qh.r()])
            dve(lambda e, o=ql.ap[:, 0, 0:Tt], i0=qf.ap[:, 0, 0:Tt], i1=qh.ap[:, 0, 0:Tt]: e.tensor_tensor(o, i0, i1, ALU.subtract),
                [qf.r(), qh.r()], [ql.r()])
            mm_group([(PS[5][:, 0:Tt], PMB.ap[:, 0, :], qh.ap[:, 0, 0:Tt], True, False),
                      (PS[5][:, 0:Tt], PMB.ap[:, 0, :], ql.ap[:, 0, 0:Tt], False, True)], [qh.r(), ql.r(), PMB.r()], 5)
            dve(lambda e, o=t1.ap[:, 0, 0:Tt], i0=qf.ap[:, 0, 0:Tt], i1=COS.ap[:, 0, 0:Tt]: e.tensor_tensor(o, i0, i1, ALU.mult),
                [qf.r(), COS.r()], [t1.r()])
            dve(lambda e, o=t2.ap[:, 0, 0:Tt], i0=PS[5][:, 0:Tt], i1=SIN.ap[:, 0, 0:Tt]: e.tensor_tensor(o, i0, i1, ALU.mult),
                [psr(5), SIN.r()], [t2.r()])
            fin(t1, t2)

        for idx, (lst, reads, fin) in enumerate(gen):
            bank = next_bank()
            lst = [(PS[bank][:, 0:Tt], l_, r_, st, sp) for (_, l_, r_, st, sp) in lst]
            mm_group(lst, reads, bank)
            qf, t1, t2 = QF[idx % 2], T1[idx % 2], T2[idx % 2]
            act(lambda e, o=qf.ap[:, 0, 0:Tt], i=PS[bank][:, 0:Tt]: e.activation(out=o, in_=i, func=AF.Copy),
                [psr(bank)], [qf.r()])
            if pending is not None:
                finish(pending)
            pending = (qf, t1, t2, fin, QH[idx % 2], QL[idx % 2])
        if pending is not None:
            finish(pending)

    def kv_phase(Tt, segs, carry, vblocks, kouts, vouts):
        if carry is not None:
            ksrc, vsrc, scaled = carry
            for KX in (KTE, KTO):
                act(lambda e, o=KX.ap[:, :, 0:128], i=KX.ap[:, :, ksrc:ksrc + 128]: e.activation(out=o, in_=i, func=AF.Copy),
                    [KX.r()], [KX.r()])
            for d_, s_ in enumerate(vsrc):
                if scaled:
                    dve(lambda e, o=VB.ap[:, d_, :, :], i=VB.ap[:, s_, :, :]: e.tensor_scalar(o, i, HV.ap[:, 0, :], None, ALU.mult),
                        [VB.r(s_), HV.r()], [VB.r(d_)])
                else:
                    dve(lambda e, o=VB.ap[:, d_, :, :], i=VB.ap[:, s_, :, :]: e.tensor_copy(o, i), [VB.r(s_)], [VB.r(d_)])
        prenorm(Tt, segs, lambda j: ak_idx(j), lambda j: bk_idx(j), hb_dest)

        def k_fin_factory(g):
            def fin(t1, t2):
                dve(lambda e, o=t1.ap[:, 0, 0:Tt], i0=t1.ap[:, 0, 0:Tt], i1=t2.ap[:, 0, 0:Tt]: e.tensor_tensor(o, i0, i1, ALU.add),
                    [t1.r(), t2.r()], [t1.r()])
                act(lambda e, o=KTE.ap[0:64, g, 128:128 + Tt], i=t1.ap[0:64, 0, 0:Tt]: e.activation(out=o, in_=i, func=AF.Copy),
                    [t1.r()], [KTE.r(g)])
                act(lambda e, o=KTO.ap[64:128, g, 128:128 + Tt], i=t1.ap[64:128, 0, 0:Tt]: e.activation(out=o, in_=i, func=AF.Copy),
                    [t1.r()], [KTO.r(g)])
                for (src0, n, dst0) in kouts:
                    dve(lambda e, o=KOUT.ap[:, g, dst0:dst0 + n], i=t1.ap[0:64, 0, src0:src0 + n]: e.tensor_copy(o, i),
                        [t1.r()], [KOUT.r(g)])
            return fin

        def gen_k():
            cur_slot = None
            for g in range(4):
                if g % 2 == 0:
                    cur_slot = load_slab(wkd[:, (g // 2) * SCW:(g // 2 + 1) * SCW])
                mm = g % 2
                lst = [(None, cur_slot.ap[:, k, mm * 128:(mm + 1) * 128], HB.ap[:, k, 0:Tt], k == 0, k == NK - 1) for k in range(NK)]
                yield (lst, [cur_slot.r(), HB.r()], k_fin_factory(g))

        if SUB >= 1:
            rope_chunks_gen(gen_k(), Tt)
        if SUB < 2:
            return
        slot = load_slab(wv[:, :])
        for (c0, n, blk) in vblocks:
            bank = next_bank()
            lst = [(PS[bank][0:n, 0:256], HB.ap[:, k, c0:c0 + n], slot.ap[:, k, 0:256], k == 0, k == NK - 1) for k in range(NK)]
            mm_group(lst, [slot.r(), HB.r()], bank)
            psv = PS[bank][0:n, 0:256].rearrange("p (g d) -> p g d", d=64)
            if SUB < 3:
                continue
            act(lambda e, o=VB.ap[0:n, blk, :, 0:64], i=psv: e.activation(out=o, in_=i, func=AF.Copy), [psr(bank)], [VB.r(blk)])
            act(lambda e, o=VB.ap[0:n, blk, :, 64:128], i=psv: e.activation(out=o, in_=i, func=AF.Copy), [psr(bank)], [VB.r(blk)])
            for (sc0, oblk) in vouts:
                if sc0 == c0 and SUB >= 4:
                    act(lambda e, o=VOUT.ap[0:n, oblk, :], i=PS[bank][0:n, 0:256]: e.activation(out=o, in_=i, func=AF.Copy),
                        [psr(bank)], [VOUT.r(oblk)])

    xT_v = xT.rearrange("(k p) n -> p k n", p=128)
    yT_v = yT.rearrange("(k p) n -> p k n", p=128)

    def run_tile(ti):
        pg.phase(f"t{ti}")
        is_hs = ti == 0
        if is_hs:
            Tt, col0 = TH, 0
            segs = [dict(c0=0, n=HALO, j=0, pfx="PF"), dict(c0=HALO, n=NSS, j=1, pfx=0), dict(c0=HALO + NSS, n=NSS, j=2, pfx=1)]
            vblocks = [(0, 64, 2), (64, 64, 3), (128, 64, 4), (192, 32, 5), (224, 32, 6)]
            carry = None
            kouts = [(HALO, 2 * NSS, 128)]
            vouts = [(192, 2), (224, 3)]
        else:
            Tt, col0 = TM, TH + (ti - 1) * TM
            segs = [dict(c0=0, n=TM, j=0, pfx="PF")]
            vblocks = [(64 * i, 64, 2 + i) for i in range(8)]
            carry = (192, [3, 4], True) if ti == 1 else (512, [8, 9], False)
            kouts = [(384, 128, 0)] if ti == NMAIN else []
            vouts = [(384, 0), (448, 1)] if ti == NMAIN else []
        pg.op("sync", lambda e: e.dma_start(out=X.ap[:, :, 0:Tt], in_=xT_v[:, :, col0:col0 + Tt]), reads=[], writes=[X.r()], dsem=s_x)
        pg.op("sync", lambda e: e.dma_start(out=COS.ap[:, 0, 0:Tt], in_=cosT[:, col0:col0 + Tt]), reads=[], writes=[COS.r()], dsem=s_tab)
        pg.op("sync", lambda e: e.dma_start(out=SIN.ap[:, 0, 0:Tt], in_=sinT[:, col0:col0 + Tt]), reads=[], writes=[SIN.r()], dsem=s_tab)
        tv = pg.cnt[s_tab]
        for b in (COS, SIN):
            for seg in pg.maps["sb"].cover(b.off, b.off + b.nbytes):
                seg[2] = (s_tab, tv)
        blocks = []
        if is_hs:
            for s in range(2):
                q0 = HALO + NSS * s
                for g in range(4):
                    pieces = []
                    for pc in range(2):
                        pieces.append((CKTE.ap[:, s, g, pc * 64:(pc + 1) * 64], CKTO.ap[:, s, g, pc * 64:(pc + 1) * 64],
                                       [CKTE.r(), CKTO.r()], CVB.ap[:, s, pc, g, :], CVB.r(), VONE.ap[:, 0, :], VONE.r(), 64))
                    pieces.append((KTE.ap[:, g, 128 + q0:128 + q0 + NSS], KTO.ap[:, g, 128 + q0:128 + q0 + NSS],
                                   [KTE.r(g), KTO.r(g)], VB.ap[:, 5 + s, g, :], VB.r(5 + s), VONE32.ap[:, 0, :], VONE32.r(), NSS))
                    blocks.append((q0, NSS, g, pieces))
        else:
            for c in range(8):
                for g in range(4):
                    pieces = []
                    for pc in range(3):
                        blk = c + pc
                        val = VHV if (ti == 1 and blk < 2) else VONE
                        pieces.append((KTE.ap[:, g, blk * 64:(blk + 1) * 64], KTO.ap[:, g, blk * 64:(blk + 1) * 64],
                                       [KTE.r(g), KTO.r(g)], VB.ap[:, blk, g, :], VB.r(blk), val.ap[:, 0, :], val.r(), 64))
                    blocks.append((c * 64, 64, g, pieces))
        for l in range(DEPTH):
            if l < 2:
                if gate(10 * ti + 3 * l + 1):
                    pool_layer(l, Tt, segs, first_main=(ti == 1), is_hs=is_hs)
            else:
                if gate(10 * ti + 3 * l + 1):
                    attn_layer(l, Tt, segs, blocks)
            if gate(10 * ti + 3 * l + 2):
                mlp_layer(l, Tt, segs)
            if l == 1:
                if gate(10 * ti + 3 * l + 3):
                    kv_phase(Tt, segs, carry, vblocks, kouts, vouts)
        if is_hs:
            pg.op("sync", lambda e: e.dma_start(out=yT_v[:, :, 0:2 * NSS], in_=X.ap[:, :, HALO:TH]), reads=[X.r()], writes=[], dsem=s_y)
        else:
            oc = 2 * NSS + (ti - 1) * TM
            pg.op("sync", lambda e: e.dma_start(out=yT_v[:, :, oc:oc + TM], in_=X.ap[:, :, 0:TM]), reads=[X.r()], writes=[], dsem=s_y)

    for ti in range(NMAIN + 1):
        run_tile(ti)

    for l in range(2):
        pv = poolT[l].rearrange("(k p) r -> p k r", p=128)
        pg.op("sync", lambda e, o=pv[:, :, 0:15], i=PF.ap[:, l, :, :]: e.dma_start(out=o, in_=i), reads=[PF.r(l)], writes=[], dsem=s_out)
        for s in range(2):
            pg.op("sync", lambda e, o=pv[:, :, 15 + 15 * s:30 + 15 * s], i=SPF.ap[:, l, :, s, :]: e.dma_start(out=o, in_=i),
                  reads=[SPF.r(l)], writes=[], dsem=s_out)
    pg.op("sync", lambda e: e.dma_start(out=kout.rearrange("g d n -> d g n"), in_=KOUT.ap), reads=[KOUT.r()], writes=[], dsem=s_out)
    pg.op("sync", lambda e: e.dma_start(out=vout.rearrange("b t f -> t b f"), in_=VOUT.ap), reads=[VOUT.r()], writes=[], dsem=s_out)
    finals = [(s_y, pg.cnt[s_y]), (s_out, pg.cnt[s_out])]

    with nc.Block() as block:
        @block.tensor
        def _(e):
            pg.emit("pe", e)

        @block.scalar
        def _(e):
            pg.emit("act", e)

        @block.vector
        def _(e):
            pg.emit("dve", e)

        @block.gpsimd
        def _(e):
            pg.emit("pool", e)

        @block.sync
        def _(e):
            pg.emit("sync", e, final_waits=finals)
    return nc


def _fm(v):
    v = np.asarray(v, np.float32)
    lead = v.shape[:-1]
    a = v.reshape(lead + (NK, 128))
    a = np.moveaxis(a, -1, 0)
    return np.ascontiguousarray(a)


_NC_CACHE = {}


def kernel(**inputs):
    in_maps = _prepare(**inputs)
    if "nc" not in _NC_CACHE:
        _NC_CACHE["nc"] = build_program()
    nc = _NC_CACHE["nc"]
    res = run_bass_kernel_spmd(nc, in_maps, core_ids=list(range(8)))
    return _assemble(res.results)


def _prepare(x_prompt, x_sample, c_prompt, c_sample, state_pool, cache_k, cache_v,
             w_mod, b_mod, g_norm, w_pool, pool_scale, w_kv_mod, b_kv_mod, g_kv, w_kv,
             w_q, sinks, w_o, w_up, w_down):
    f32 = np.float32
    x_prompt = np.asarray(x_prompt, f32)
    x_sample = np.asarray(x_sample, f32)
    c_prompt = np.asarray(c_prompt, f32)
    c_sample = np.asarray(c_sample, f32)
    state_pool = np.asarray(state_pool, f32)
    cache_k = np.asarray(cache_k, f32)
    cache_v = np.asarray(cache_v, f32)
    w_mod = np.ascontiguousarray(np.asarray(w_mod, f32))
    w_kv_mod = np.ascontiguousarray(np.asarray(w_kv_mod, f32))
    w_kv = np.asarray(w_kv, f32)
    w_q = np.ascontiguousarray(np.asarray(w_q, f32))
    w_o = np.ascontiguousarray(np.asarray(w_o, f32))
    w_up = np.ascontiguousarray(np.asarray(w_up, f32))
    w_down = np.ascontiguousarray(np.asarray(w_down, f32))
    sinks = np.asarray(sinks, f32)


    gT = _fm(np.asarray(g_norm, f32).reshape(16, D))
    psT = _fm(np.asarray(pool_scale, f32))
    bmodT = np.ascontiguousarray(np.moveaxis(np.asarray(b_mod, f32).reshape(4, 96, 128), -1, 0))
    bkvT = np.ascontiguousarray(np.asarray(b_kv_mod, f32).reshape(32, 128).T)
    gkvT = np.ascontiguousarray(np.asarray(g_kv, f32).reshape(NK, 128).T)
    w_pool2 = np.ascontiguousarray(np.asarray(w_pool, f32).reshape(2, D, 512))
    wk = w_kv[:, :256].reshape(D, 4, 1, 64)
    wkd = np.ascontiguousarray(np.broadcast_to(wk, (D, 4, 2, 64)).reshape(D, 512))
    wv = np.ascontiguousarray(w_kv[:, 256:])
    sk = np.ascontiguousarray(sinks.reshape(2, 4, 4, 2).transpose(0, 1, 3, 2).reshape(1, 64))
    pmat = np.zeros((128, 128), f32)
    for m in range(128):
        d = m % 64
        partner = m + 32 if d < 32 else m - 32
        pmat[partner, m] = 1.0
    inv = (np.float32(10000.0) ** (-(np.arange(32, dtype=f32) / np.float32(32)))).astype(f32)

    in_maps = []
    for c in range(8):
        b, half = c // 2, c % 2
        m0 = half * 2048
        xT = np.zeros((D, NCOLS), f32)
        pos = np.zeros((NCOLS,), f32)
        if half == 1:
            xT[:, 0:HALO] = x_prompt[b, m0 - HALO:m0].T
            pos[0:HALO] = np.arange(m0 - HALO, m0)
        for s in range(2):
            xT[:, HALO + NSS * s:HALO + NSS * (s + 1)] = x_sample[2 * c + s].T
            pos[HALO + NSS * s:HALO + NSS * (s + 1)] = 4096 + np.arange(NSS)
        xT[:, TH:] = x_prompt[b, m0:m0 + NMAIN * TM].T
        pos[TH:] = np.arange(m0, m0 + NMAIN * TM)
        ang = pos.astype(f32)[None, :] * inv[:, None]
        cs = np.cos(ang).astype(f32)
        sn = np.sin(ang).astype(f32)
        cos64 = np.concatenate([cs, cs], 0)
        sin64 = np.concatenate([-sn, sn], 0)
        cosT = np.ascontiguousarray(np.concatenate([cos64, cos64], 0))
        sinT = np.ascontiguousarray(np.concatenate([sin64, sin64], 0))
        cvecs = np.stack([c_prompt[b], c_sample[2 * c], c_sample[2 * c + 1]], 0)
        cTt = np.ascontiguousarray(cvecs.reshape(3, NK, 128).transpose(2, 1, 0))
        invc = np.zeros((128, 4, 16), f32)
        for g in range(4):
            w = 2 << g
            cnt = np.minimum(w, m0 + np.arange(16) + 1).astype(f32)
            invc[:, g, :] = (np.float32(1.0) / cnt)[None, :]
        hvv = np.full((128, 1), float(half), f32)
        sp = state_pool[:, 2 * c:2 * c + 2]
        spT = np.ascontiguousarray(sp.reshape(2, 2, 15, NK, 128).transpose(4, 0, 3, 1, 2))
        ck = cache_k[2 * c:2 * c + 2]
        ckt = ck.transpose(3, 0, 2, 1)
        ckT = np.ascontiguousarray(np.concatenate([ckt, ckt], 0))
        cv = cache_v[2 * c:2 * c + 2].reshape(2, 2, 64, 4, 64)
        cvt = cv.transpose(2, 0, 1, 3, 4)
        cvT = np.ascontiguousarray(np.concatenate([cvt, cvt], -1))
        in_maps.append(dict(
            xT=xT, cT=cTt, gT=gT, psT=psT, bmodT=bmodT, bkvT=bkvT, gkvT=gkvT, cosT=cosT, sinT=sinT, invc=invc, hv=hvv,
            pm=pmat, sk=sk, spT=spT, ckT=ckT, cvT=cvT, w_mod=w_mod, w_kv_mod=w_kv_mod, w_pool=w_pool2, wkd=wkd, wv=wv,
            w_q=w_q, w_o=w_o, w_up=w_up, w_down=w_down))
    return in_maps


def _assemble(outs):
    f32 = np.float32

    y_prompt = np.zeros((4, 4096, D), f32)
    y_sample = np.zeros((16, 32, D), f32)
    pool_p = np.zeros((2, 4, 15, D), f32)
    pool_s = np.zeros((2, 16, 15, D), f32)
    k_p = np.zeros((4, 128, 4, 64), f32)
    v_p = np.zeros((4, 128, 4, 64), f32)
    k_s = np.zeros((16, 32, 4, 64), f32)
    v_s = np.zeros((16, 32, 4, 64), f32)
    for c in range(8):
        b, half = c // 2, c % 2
        o = outs[c]
        yT = np.asarray(o["yT"])
        y_prompt[b, half * 2048:(half + 1) * 2048] = yT[:, 2 * NSS:].T
        for s in range(2):
            y_sample[2 * c + s] = yT[:, NSS * s:NSS * (s + 1)].T
        pT = np.asarray(o["poolT"])
        ko = np.asarray(o["kout"])
        vo = np.asarray(o["vout"])
        for l in range(2):
            if half == 1:
                pool_p[l, b] = pT[l, :, 0:15].T
            for s in range(2):
                pool_s[l, 2 * c + s] = pT[l, :, 15 + 15 * s:30 + 15 * s].T
        if half == 1:
            k_p[b] = ko[:, :, 0:128].transpose(2, 0, 1)
            v_p[b] = np.concatenate([vo[0], vo[1]], 0).reshape(128, 4, 64)
        for s in range(2):
            k_s[2 * c + s] = ko[:, :, 128 + NSS * s:128 + NSS * (s + 1)].transpose(2, 0, 1)
            v_s[2 * c + s] = vo[2 + s][0:NSS].reshape(NSS, 4, 64)
    return (y_prompt, y_sample, pool_p, pool_s, k_p, v_p, k_s, v_s)
```

```python
import bisect
import os
import numpy as np
import concourse.bass as bass
import concourse.mybir as mybir
from concourse.bass_utils import run_bass_kernel_spmd

F32 = mybir.dt.float32
BF16 = mybir.dt.bfloat16
U8 = mybir.dt.uint8
AF = mybir.ActivationFunctionType
ALU = mybir.AluOpType

D = 2048
NK = 16
DFF = 8192
TM = 512
HALO = 192
NSS = 32
TH = HALO + 2 * NSS
NMAIN = int(os.environ.get("KDBG_NMAIN", "4"))
NCOLS = TH + NMAIN * TM
SCW = 256
NSLOT = 4
EPS = 1e-6
SCALE = 64 ** -0.5
SAME_SYNC = True
STAGE = int(os.environ.get("KDBG_STAGE", "1000"))
SUB = int(os.environ.get("KDBG_SUB", "1000"))
ASUB = int(os.environ.get("KDBG_ASUB", "1000"))


def gate(n):
    return n <= STAGE
DEPTH = 4


class IMap:
    def __init__(self, size):
        self.starts = [0]
        self.segs = [[0, size, None, {}]]

    def _split(self, pos):
        i = bisect.bisect_right(self.starts, pos) - 1
        seg = self.segs[i]
        if seg[0] == pos or pos >= seg[1]:
            return
        new = [pos, seg[1], seg[2], dict(seg[3])]
        seg[1] = pos
        self.starts.insert(i + 1, pos)
        self.segs.insert(i + 1, new)

    def cover(self, a, b):
        self._split(a)
        self._split(b)
        i = bisect.bisect_left(self.starts, a)
        out = []
        while i < len(self.segs) and self.segs[i][0] < b:
            out.append(self.segs[i])
            i += 1
        return out


class Prog:
    ENGS = ("pe", "act", "dve", "pool", "sync")

    def __init__(self, nc):
        self.nc = nc
        self.sems = []
        self.cnt = []
        self.q = {e: [] for e in self.ENGS}
        self.known = {e: {} for e in self.ENGS}
        self.cur = {}
        self.own = {e: set() for e in self.ENGS}
        self.maps = {"sb": IMap(1 << 20), "ps": IMap(8 * 2048)}
        self.dram = {}

    def new_sem(self, name):
        h = self.nc.alloc_semaphore(name=name)
        self.sems.append(h)
        self.cnt.append(0)
        return len(self.sems) - 1

    def phase(self, tag):
        for e in ("pe", "act", "dve"):
            s = self.new_sem(f"{e}_{tag}")
            self.cur[e] = s
            self.own[e].add(s)

    def _map(self, sp):
        if sp in self.maps:
            return self.maps[sp]
        m = self.maps[sp] = IMap(1 << 40)
        return m

    def op(self, eng, fn, reads=(), writes=(), dsem=None):
        need = {}
        for (sp, a, b) in reads:
            for seg in self._map(sp).cover(a, b):
                w = seg[2]
                if w is not None:
                    if need.get(w[0], 0) < w[1]:
                        need[w[0]] = w[1]
        for (sp, a, b) in writes:
            for seg in self._map(sp).cover(a, b):
                w = seg[2]
                if w is not None:
                    if need.get(w[0], 0) < w[1]:
                        need[w[0]] = w[1]
                for s_, v_ in seg[3].items():
                    if need.get(s_, 0) < v_:
                        need[s_] = v_
        waits = []
        kn = self.known[eng]
        for s_, v_ in need.items():
            if s_ in self.own[eng] and (eng == "pe" or not SAME_SYNC):
                continue
            if kn.get(s_, 0) >= v_:
                continue
            kn[s_] = v_
            waits.append((s_, v_))
        if dsem is None:
            sem = self.cur[eng]
            inc = 1
        else:
            sem = dsem
            inc = 16
        self.cnt[sem] += inc
        val = self.cnt[sem]
        for (sp, a, b) in reads:
            for seg in self._map(sp).cover(a, b):
                if seg[3].get(sem, 0) < val:
                    seg[3][sem] = val
        for (sp, a, b) in writes:
            for seg in self._map(sp).cover(a, b):
                seg[2] = (sem, val)
                seg[3] = {}
        self.q[eng].append((waits, fn, sem, inc))
        return (sem, val)

    def emit(self, eng, e, final_waits=()):
        sems = self.sems
        for (waits, fn, sem, inc) in self.q[eng]:
            for (s_, v_) in waits:
                e.wait_ge(sems[s_], v_)
            ins = fn(e)
            ins.then_inc(sems[sem], inc)
        for (s_, v_) in final_waits:
            e.wait_ge(sems[s_], v_)


class Buf:
    def __init__(self, big, off, parts, free, dtype, esz):
        self.off = off
        self.free = list(free)
        self.esz = esz
        n = 1
        for f in free:
            n *= f
        self.nbytes = n * esz
        ap = big[0:parts, off:off + self.nbytes].bitcast(dtype)
        if len(free) == 2:
            ap = ap.rearrange("p (a b) -> p a b", b=free[1])
        elif len(free) == 3:
            ap = ap.rearrange("p (a b c) -> p a b c", b=free[1], c=free[2])
        elif len(free) == 4:
            ap = ap.rearrange("p (a b c d) -> p a b c d", b=free[1], c=free[2], d=free[3])
        self.ap = ap
        self.sub = self.nbytes // free[0]

    def r(self, i=None, n=1):
        if i is None:
            return ("sb", self.off, self.off + self.nbytes)
        return ("sb", self.off + i * self.sub, self.off + (i + n) * self.sub)


def _esz(dt):
    return 4 if dt == F32 else 2


def build_program():
    nc = bass.Bass("TRN2", target_bir_lowering=False)
    pg = Prog(nc)

    def din(name, shape, dt=F32):
        return nc.dram_tensor(name, list(shape), dt, kind="ExternalInput").ap()

    def dout(name, shape, dt=F32):
        return nc.dram_tensor(name, list(shape), dt, kind="ExternalOutput").ap()

    xT = din("xT", [D, NCOLS])
    cT = din("cT", [128, NK, 3])
    gT = din("gT", [128, 16, NK])
    psT = din("psT", [128, 2, NK])
    bmodT = din("bmodT", [128, 4, 96])
    bkvT = din("bkvT", [128, 32])
    gkvT = din("gkvT", [128, NK])
    cosT = din("cosT", [128, NCOLS])
    sinT = din("sinT", [128, NCOLS])
    invc = din("invc", [128, 4, 16])
    hv = din("hv", [128, 1])
    pm = din("pm", [128, 128])
    sk = din("sk", [1, 64])
    spT = din("spT", [128, 2, NK, 2, 15])
    ckT = din("ckT", [128, 2, 4, 128])
    cvT = din("cvT", [64, 2, 2, 4, 128])
    w_mod = din("w_mod", [4, D, 6 * D])
    w_kv_mod = din("w_kv_mod", [D, 2 * D])
    w_pool = din("w_pool", [2, D, 512])
    wkd = din("wkd", [D, 512])
    wv = din("wv", [D, 256])
    w_q = din("w_q", [2, D, D])
    w_o = din("w_o", [2, D, D])
    w_up = din("w_up", [4, D, DFF])
    w_down = din("w_down", [4, DFF, D])

    yT = dout("yT", [D, 2 * NSS + NMAIN * TM])
    poolT = dout("poolT", [2, D, 45])
    kout = dout("kout", [4, 64, 192])
    vout = dout("vout", [4, 64, 256])

    TOTAL = 207 * 1024
    big = nc.alloc_sbuf_tensor("big", [128, TOTAL], U8)
    cursor = [0]

    def alloc(free, dt, parts=128, at=None):
        esz = _esz(dt)
        n = esz
        for f in free:
            n *= f
        if at is None:
            off = cursor[0]
            cursor[0] = (off + n + 31) // 32 * 32
        else:
            off = at
        return Buf(big, off, parts, free, dt, esz)

    X = alloc([NK, TM], F32)
    HB = alloc([NK, TM], BF16)
    A = alloc([32, TM], BF16)
    MIX = alloc([NK, TM], F32)
    SLOT = [alloc([NK, SCW], BF16) for _ in range(NSLOT)]
    QB = alloc([NK, TM], BF16, at=A.off)
    OB = alloc([NK, TM], BF16, at=A.off + 16384)
    EXTW = TM + 48
    HF = alloc([1, EXTW], F32, at=A.off)
    S0 = alloc([1, EXTW], F32, at=A.off + 4096)
    S1 = alloc([1, EXTW], F32, at=A.off + 8192)
    TMP2 = alloc([1, 16], F32, at=A.off + 12288)
    MODT = alloc([4, 96, 3], F32, at=A.off)
    MODK = alloc([32, 3], F32, at=A.off + 8192)
    G = alloc([16, NK], F32, at=A.off + 10240)
    BMOD = alloc([4, 96], F32, at=A.off + 12288)
    BKV = alloc([1, 32], F32, at=A.off + 14336)
    GKV = alloc([1, NK], F32, at=A.off + 15360)
    CT = alloc([NK, 3], F32, at=A.off + 16384)
    SKr = alloc([1, 64], F32, parts=1, at=A.off + 17408)
    QF = [alloc([1, TM], F32, at=MIX.off + i * 2048) for i in range(2)]
    T1 = [alloc([1, TM], F32, at=MIX.off + 4096 + i * 2048) for i in range(2)]
    T2 = [alloc([1, TM], F32, at=MIX.off + 8192 + i * 2048) for i in range(2)]
    PT = [alloc([3, TM], BF16, at=MIX.off + 12288 + i * 3072) for i in range(2)]
    RD = [alloc([1, TM], F32, at=MIX.off + 18432 + i * 2048) for i in range(2)]
    QH = [alloc([1, TM], BF16, at=MIX.off + 22528 + i * 1024) for i in range(2)]
    QL = [alloc([1, TM], BF16, at=MIX.off + 24576 + i * 1024) for i in range(2)]
    KTE = alloc([4, 128 + TM], BF16)
    KTO = alloc([4, 128 + TM], BF16)
    VB = alloc([10, 4, 128], BF16)
    CKTE = alloc([2, 4, 128], BF16)
    CKTO = alloc([2, 4, 128], BF16)
    CVB = alloc([2, 2, 4, 128], BF16)
    VONE = alloc([1, 128], BF16)
    VONE32 = alloc([1, 128], BF16)
    VHV = alloc([1, 128], BF16)
    E0 = alloc([1, 128], BF16)
    SQ = [alloc([1, TM], BF16) for _ in range(2)]
    TMP = [alloc([1, TM], F32) for _ in range(1)]
    SQALL = alloc([NK, TM], BF16, at=A.off)
    XR = [alloc([4, TM], F32, at=A.off + 16384 + i * 8192) for i in range(2)]
    RL = [alloc([1, TM], F32) for _ in range(1)]
    RSTD = alloc([1, TM], F32)
    COS = alloc([1, TM], F32)
    SIN = alloc([1, TM], F32)
    NPAR = 78
    PARAM = alloc([NPAR, NK], F32)
    PSC = alloc([2, NK], F32)
    PF = alloc([2, NK, 15], F32)
    SPF = alloc([2, NK, 2, 15], F32)
    KOUT = alloc([4, 192], F32, parts=64)
    VOUT = alloc([4, 256], F32, parts=64)
    PM = alloc([1, 128], F32)
    ONESD = alloc([1, 128], BF16)
    ONESF = alloc([1, 128], F32, parts=1)
    SKE = alloc([1, 64], F32, parts=1)
    INVC = alloc([4, 16], F32)
    HV = alloc([1, 1], F32)
    EPSB = alloc([1, 1], F32)
    SC3 = alloc([NK, 3], BF16)
    PMB = alloc([1, 128], BF16)
    SKH = alloc([1, 64], BF16)
    SKL = alloc([1, 64], BF16)
    SKT = alloc([1, 64], F32, parts=1)
    assert cursor[0] <= TOTAL, cursor[0]

    PS = [nc.alloc_psum_tensor(f"ps{i}", [128, 512], F32) for i in range(8)]

    def psr(b):
        return ("ps", b * 2048, (b + 1) * 2048)

    def par_idx(l, kind, j):
        return (l * 4 + kind) * 3 + j

    def ak_idx(j):
        return 48 + j

    def b_idx(l, which, j):
        return 51 + (l * 2 + which) * 3 + j

    def bk_idx(j):
        return 51 + 24 + j

    pg.phase("setup")
    s_setup = pg.new_sem("setup")
    s_setup2 = pg.new_sem("setup2")
    s_slot = [pg.new_sem(f"slot{i}") for i in range(NSLOT)]
    s_x = pg.new_sem("xload")
    s_tab = pg.new_sem("tab")
    s_y = pg.new_sem("ystore")
    s_out = pg.new_sem("outs")

    slab_ctr = [0]

    def load_slab(src2d, ncols=SCW):
        i = slab_ctr[0] % NSLOT
        slab_ctr[0] += 1
        slot = SLOT[i]
        src = src2d.rearrange("(k p) n -> p k n", p=128)
        dst = slot.ap[:, :, 0:ncols]
        pg.op("pool", lambda e, d=dst, s=src: e.dma_start(out=d, in_=s), reads=[], writes=[slot.r()], dsem=s_slot[i])
        return slot

    bank_ctr = [0]

    def next_bank():
        b = bank_ctr[0] % 4
        bank_ctr[0] += 1
        return b

    def mm_group(lst, reads, bank):
        def fn(e, lst=lst):
            last = None
            for (o, l, r, st, sp) in lst:
                last = e.matmul(o, l, r, start=st, stop=sp)
            return last
        pg.op("pe", fn, reads=reads, writes=[psr(bank)])

    setup_loads = []

    def sload(eng, dst_buf, dst_ap, src_ap):
        pg.op(eng, lambda e, d=dst_ap, s=src_ap: e.dma_start(out=d, in_=s), reads=[], writes=[dst_buf.r()],
              dsem=(s_setup if eng == "sync" else s_setup2))

    sload("sync", CT, CT.ap, cT)
    sload("sync", G, G.ap, gT)
    sload("sync", PSC, PSC.ap, psT)
    sload("sync", BMOD, BMOD.ap, bmodT)
    sload("sync", BKV, BKV.ap[:, 0, :], bkvT)
    sload("sync", GKV, GKV.ap[:, 0, :], gkvT)
    sload("sync", INVC, INVC.ap, invc)
    sload("sync", HV, HV.ap[:, 0, :], hv)
    sload("sync", PM, PM.ap[:, 0, :], pm)
    sload("sync", SKr, SKr.ap[:, 0, :], sk)
    sload("sync", SPF, SPF.ap, spT)
    tot = pg.cnt[s_setup]
    for b in (CT, G, PSC, BMOD, BKV, GKV, INVC, HV, PM, SKr, SPF):
        for seg in pg.maps["sb"].cover(b.off, b.off + b.nbytes):
            seg[2] = (s_setup, tot)

    def dve(fn, reads, writes):
        pg.op("dve", fn, reads=reads, writes=writes)

    def act(fn, reads, writes):
        pg.op("act", fn, reads=reads, writes=writes)

    dve(lambda e: e.memset(ONESD.ap[:, 0, :], 1.0 / D), [], [ONESD.r()])
    dve(lambda e: e.memset(ONESF.ap[:, 0, :], 1.0), [], [ONESF.r()])
    for cbuf in (VONE, VONE32, VHV, E0, SKH, SKL, CKTE, CKTO, CVB, KTO):
        dve(lambda e, c=cbuf: e.memset(c.ap, 0.0), [], [cbuf.r()])
    dve(lambda e: e.memset(VONE.ap[0:64, 0, :], 1.0), [], [VONE.r()])
    dve(lambda e: e.memset(VONE32.ap[0:32, 0, :], 1.0), [], [VONE32.r()])
    dve(lambda e: e.memset(E0.ap[0:1, 0, :], 1.0), [], [E0.r()])
    pg.op("pool", lambda e: e.dma_start(out=CKTE.ap[0:64], in_=ckT[0:64]), reads=[], writes=[CKTE.r()], dsem=s_setup2)
    pg.op("pool", lambda e: e.dma_start(out=CKTO.ap[64:128], in_=ckT[64:128]), reads=[], writes=[CKTO.r()], dsem=s_setup2)
    pg.op("pool", lambda e: e.dma_start(out=CVB.ap[0:64], in_=cvT), reads=[], writes=[CVB.r()], dsem=s_setup2)
    tot2 = pg.cnt[s_setup2]
    for b in (CKTE, CKTO, CVB):
        for seg in pg.maps["sb"].cover(b.off, b.off + b.nbytes):
            seg[2] = (s_setup2, tot2)
    dve(lambda e: e.memset(PF.ap, 0.0), [], [PF.r()])
    dve(lambda e: e.memset(EPSB.ap[:, 0, :], EPS), [], [EPSB.r()])
    dve(lambda e: e.memset(VOUT.ap, 0.0), [], [VOUT.r()])
    dve(lambda e: e.memset(KOUT.ap, 0.0), [], [KOUT.r()])
    dve(lambda e: e.memset(VB.ap, 0.0), [], [VB.r()])
    dve(lambda e: e.memset(KTE.ap, 0.0), [], [KTE.r()])
    dve(lambda e: e.tensor_scalar(VHV.ap[:, 0, :], VONE.ap[:, 0, :], HV.ap[:, 0, :], None, ALU.mult),
        [VONE.r(), HV.r()], [VHV.r()])
    act(lambda e: e.activation(out=SC3.ap, in_=CT.ap, func=AF.Silu), [CT.r()], [SC3.r()])
    act(lambda e: e.activation(out=SKE.ap[:, 0, :], in_=SKr.ap[:, 0, :], func=AF.Exp), [SKr.r()], [SKE.r()])
    dve(lambda e: e.tensor_copy(PMB.ap[:, 0, :], PM.ap[:, 0, :]), [PM.r()], [PMB.r()])
    dve(lambda e: e.tensor_copy(SKH.ap[0:1, 0, :], SKE.ap[:, 0, :]), [SKE.r()], [SKH.r()])
    dve(lambda e: e.tensor_tensor(SKT.ap[:, 0, :], SKE.ap[:, 0, :], SKH.ap[0:1, 0, :], ALU.subtract), [SKE.r(), SKH.r()], [SKT.r()])
    dve(lambda e: e.tensor_copy(SKL.ap[0:1, 0, :], SKT.ap[:, 0, :]), [SKT.r()], [SKL.r()])

    def mod_pass(wsrc, ncol, dst_fn, bias_fn):
        for s in range(ncol // SCW):
            slot = load_slab(wsrc[:, s * SCW:(s + 1) * SCW])
            bank = 4 + (s % 2)
            lst = []
            for mm in range(2):
                for k in range(NK):
                    lst.append((PS[bank][:, mm * 3:(mm + 1) * 3], slot.ap[:, k, mm * 128:(mm + 1) * 128],
                                SC3.ap[:, k, :], k == 0, k == NK - 1))
            mm_group(lst, [slot.r(), SC3.r()], bank)
            psv = PS[bank][:, 0:6].rearrange("p (a b) -> p a b", b=3)
            for j in range(3):
                d_ap, d_r = dst_fn(s, j)
                dve(lambda e, o=d_ap, i0=psv[:, :, j], i1=bias_fn(s): e.tensor_tensor(o, i0, i1, ALU.add),
                    [psr(bank), BMOD.r(), BKV.r()], [d_r])

    for l in range(DEPTH if gate(-1) else 0):
        mod_pass(w_mod[l], 6 * D,
                 lambda s, j, l=l: (MODT.ap[:, l, 2 * s:2 * s + 2, j], MODT.r(l)),
                 lambda s, l=l: BMOD.ap[:, l, 2 * s:2 * s + 2])
    if gate(-1):
        mod_pass(w_kv_mod, 2 * D,
                 lambda s, j: (MODK.ap[:, 2 * s:2 * s + 2, j], MODK.r()),
                 lambda s: BKV.ap[:, 0, 2 * s:2 * s + 2])

    for l in range(DEPTH):
        for j in range(3):
            for (kind, sc_off, gi, plus1) in ((0, 16, 0, True), (1, 32, 1, False), (2, 64, 2, True), (3, 80, 3, False)):
                o = PARAM.ap[:, par_idx(l, kind, j), :]
                i0 = MODT.ap[:, l, sc_off:sc_off + 16, j]
                i1 = G.ap[:, l * 4 + gi, :]
                if plus1:
                    dve(lambda e, o=o, i0=i0, i1=i1: e.scalar_tensor_tensor(o, i0, 1.0, i1, ALU.add, ALU.mult),
                        [MODT.r(l), G.r()], [PARAM.r(par_idx(l, kind, j))])
                else:
                    dve(lambda e, o=o, i0=i0, i1=i1: e.tensor_tensor(o, i0, i1, ALU.mult),
                        [MODT.r(l), G.r()], [PARAM.r(par_idx(l, kind, j))])
            for which, off in ((0, 0), (1, 48)):
                o = PARAM.ap[:, b_idx(l, which, j), :]
                i0 = MODT.ap[:, l, off:off + 16, j]
                dve(lambda e, o=o, i0=i0: e.tensor_copy(o, i0), [MODT.r(l)], [PARAM.r(b_idx(l, which, j))])
    for j in range(3):
        o = PARAM.ap[:, ak_idx(j), :]
        dve(lambda e, o=o, i0=MODK.ap[:, 16:32, j], i1=GKV.ap[:, 0, :]: e.scalar_tensor_tensor(o, i0, 1.0, i1, ALU.add, ALU.mult),
            [MODK.r(), GKV.r()], [PARAM.r(ak_idx(j))])
        o2 = PARAM.ap[:, bk_idx(j), :]
        dve(lambda e, o=o2, i0=MODK.ap[:, 0:16, j]: e.tensor_copy(o, i0), [MODK.r()], [PARAM.r(bk_idx(j))])

    def mean_sq(src, Tt):
        bank = 4
        order = [(0, "act"), (8, "dve"), (4, "act"), (12, "dve")]
        for gi, (k0, eng) in enumerate(order):
            o_ap = SQALL.ap[:, k0:k0 + 4, 0:Tt]
            i_ap = src.ap[:, k0:k0 + 4, 0:Tt]
            if eng == "act":
                act(lambda e, o=o_ap, i=i_ap: e.activation(out=o, in_=i, func=AF.Square), [src.r(k0, 4)], [SQALL.r(k0, 4)])
            else:
                dve(lambda e, o=o_ap, i=i_ap: e.tensor_tensor(o, i, i, ALU.mult), [src.r(k0, 4)], [SQALL.r(k0, 4)])
            lst = [(PS[bank][:, 0:Tt], ONESD.ap[:, 0, :], SQALL.ap[:, k0 + kk, 0:Tt], gi == 0 and kk == 0, gi == 3 and kk == 3)
                   for kk in range(4)]
            mm_group(lst, [SQALL.r(k0, 4), ONESD.r()], bank)
        act(lambda e: e.activation(out=RSTD.ap[:, 0, 0:Tt], in_=PS[bank][:, 0:Tt], func=AF.Ln, bias=EPSB.ap[:, 0, :], scale=1.0),
            [psr(bank), EPSB.r()], [RSTD.r()])
        act(lambda e: e.activation(out=RSTD.ap[:, 0, 0:Tt], in_=RSTD.ap[:, 0, 0:Tt], func=AF.Exp, scale=-0.5),
            [RSTD.r()], [RSTD.r()])

    def prenorm(Tt, segs, a_idx, b_idx_fn, dest_fn, per_chunk_post=None):
        mean_sq(X, Tt)
        for grp in range(4):
            xr = XR[grp % 2]
            dve(lambda e, o=xr.ap[:, :, 0:Tt], i0=X.ap[:, 4 * grp:4 * grp + 4, 0:Tt],
                i1=RSTD.ap[:, 0:1, 0:Tt].to_broadcast([128, 4, Tt]): e.tensor_tensor(o, i0, i1, ALU.mult),
                [X.r(4 * grp, 4), RSTD.r()], [xr.r()])
            for kk in range(4):
                k = 4 * grp + kk
                for si, seg in enumerate(segs):
                    c0, n, j = seg["c0"], seg["n"], seg["j"]
                    d_ap, d_r = dest_fn(k, si, seg)
                    bi = b_idx_fn(j)
                    ai = a_idx(j)
                    act(lambda e, o=d_ap, i=xr.ap[:, kk, c0:c0 + n], sc=PARAM.ap[:, ai, k:k + 1], b=PARAM.ap[:, bi, k:k + 1]:
                        e.activation(out=o, in_=i, func=AF.Identity, bias=b, scale=sc),
                        [xr.r(kk), PARAM.r(ai), PARAM.r(bi)], [d_r])
                if per_chunk_post is not None:
                    per_chunk_post(k)

    def hb_dest(k, si, seg):
        return HB.ap[:, k, seg["c0"]:seg["c0"] + seg["n"]], HB.r(k)

    def postnorm_update(Tt, segs, gg_idx):
        mean_sq(MIX, Tt)
        for grp in range(4):
            xr = XR[grp % 2]
            dve(lambda e, o=xr.ap[:, :, 0:Tt], i0=MIX.ap[:, 4 * grp:4 * grp + 4, 0:Tt],
                i1=RSTD.ap[:, 0:1, 0:Tt].to_broadcast([128, 4, Tt]): e.tensor_tensor(o, i0, i1, ALU.mult),
                [MIX.r(4 * grp, 4), RSTD.r()], [xr.r()])
            for kk in range(4):
                k = 4 * grp + kk
                for seg in segs:
                    c0, n, j = seg["c0"], seg["n"], seg["j"]
                    dve(lambda e, o=X.ap[:, k, c0:c0 + n], i0=xr.ap[:, kk, c0:c0 + n], s=PARAM.ap[:, gg_idx(j), k:k + 1]:
                        e.scalar_tensor_tensor(o, i0, s, o, ALU.mult, ALU.add),
                        [xr.r(kk), PARAM.r(gg_idx(j)), X.r(k)], [X.r(k)])

    def pool_layer(l, Tt, segs, first_main, is_hs):
        eo = [seg["c0"] + 15 * si for si, seg in enumerate(segs)]

        def dest(k, si, seg):
            return HF.ap[:, 0, eo[si] + 15:eo[si] + 15 + seg["n"]], HF.r()

        def pre_chunk(k):
            for si, seg in enumerate(segs):
                if seg["pfx"] == "PF":
                    src, sr = PF.ap[:, l, k, :], PF.r(l)
                else:
                    src, sr = SPF.ap[:, l, k, seg["pfx"], :], SPF.r(l)
                dve(lambda e, o=HF.ap[:, 0, eo[si]:eo[si] + 15], i=src: e.tensor_copy(o, i), [sr], [HF.r()])

        def post_chunk(k):
            g = k // 4
            w = 2 << g
            for si, seg in enumerate(segs):
                n = seg["n"]
                cur = HF
                lo = eo[si]
                hi = eo[si] + 15 + n
                for step in range(g + 1):
                    sh = 1 << step
                    dst = S0 if step % 2 == 0 else S1
                    r0 = lo + (2 << step) - 1
                    dve(lambda e, o=dst.ap[:, 0, r0:hi], i0=cur.ap[:, 0, r0:hi], i1=cur.ap[:, 0, r0 - sh:hi - sh]:
                        e.tensor_tensor(o, i0, i1, ALU.add), [cur.r()], [dst.r()])
                    cur = dst
                c0 = seg["c0"]
                dve(lambda e, o=HB.ap[:, k, c0:c0 + n], i0=cur.ap[:, 0, lo + 15:hi], i1=HF.ap[:, 0, lo + 15:hi], w=w:
                    e.scalar_tensor_tensor(o, i0, 1.0 / w, i1, ALU.mult, ALU.subtract),
                    [cur.r(), HF.r()], [HB.r(k)])
                if first_main:
                    dve(lambda e, o=TMP2.ap[:, 0, :], i0=cur.ap[:, 0, lo + 15:lo + 31], i1=INVC.ap[:, g, :]:
                        e.tensor_tensor(o, i0, i1, ALU.mult), [cur.r(), INVC.r()], [TMP2.r()])
                    dve(lambda e, o=HB.ap[:, k, c0:c0 + 16], i0=TMP2.ap[:, 0, :], i1=HF.ap[:, 0, lo + 15:lo + 31]:
                        e.tensor_tensor(o, i0, i1, ALU.subtract), [TMP2.r(), HF.r()], [HB.r(k)])
                if seg["pfx"] == "PF":
                    if is_hs:
                        dve(lambda e, o=PF.ap[:, l, k, :], i=HF.ap[:, 0, hi - 15:hi]:
                            e.tensor_scalar(o, i, HV.ap[:, 0, :], None, ALU.mult), [HF.r(), HV.r()], [PF.r(l)])
                    else:
                        dve(lambda e, o=PF.ap[:, l, k, :], i=HF.ap[:, 0, hi - 15:hi]: e.tensor_copy(o, i), [HF.r()], [PF.r(l)])
                else:
                    dve(lambda e, o=SPF.ap[:, l, k, seg["pfx"], :], i=HF.ap[:, 0, hi - 15:hi]: e.tensor_copy(o, i),
                        [HF.r()], [SPF.r(l)])

        mean_sq(X, Tt)
        for k in range(NK):
            pre_chunk(k)
            tmp = TMP[0]
            for si, seg in enumerate(segs):
                c0, n, j = seg["c0"], seg["n"], seg["j"]
                dve(lambda e, o=tmp.ap[:, 0, c0:c0 + n], i0=X.ap[:, k, c0:c0 + n], s=PARAM.ap[:, par_idx(l, 0, j), k:k + 1],
                    i1=RSTD.ap[:, 0, c0:c0 + n]: e.scalar_tensor_tensor(o, i0, s, i1, ALU.mult, ALU.mult),
                    [X.r(k), PARAM.r(par_idx(l, 0, j)), RSTD.r()], [tmp.r()])
            for si, seg in enumerate(segs):
                c0, n, j = seg["c0"], seg["n"], seg["j"]
                d_ap, d_r = dest(k, si, seg)
                bi = b_idx(l, 0, j)
                act(lambda e, o=d_ap, i=tmp.ap[:, 0, c0:c0 + n], b=PARAM.ap[:, bi, k:k + 1]:
                    e.activation(out=o, in_=i, func=AF.Identity, bias=b, scale=1.0),
                    [tmp.r(), PARAM.r(bi)], [d_r])
            post_chunk(k)
        for cb in range(2):
            slot = load_slab(w_pool[l][:, cb * SCW:(cb + 1) * SCW])
            for g in range(4):
                for mm in range(2):
                    m = g * 4 + cb * 2 + mm
                    bank = next_bank()
                    lst = [(PS[bank][:, 0:Tt], slot.ap[:, g * 4 + kk, mm * 128:(mm + 1) * 128], HB.ap[:, g * 4 + kk, 0:Tt],
                            kk == 0, kk == 3) for kk in range(4)]
                    mm_group(lst, [slot.r(), HB.r(g * 4, 4)], bank)
                    dve(lambda e, o=MIX.ap[:, m, 0:Tt], i=PS[bank][:, 0:Tt], s=PSC.ap[:, l, m:m + 1]:
                        e.tensor_scalar(o, i, s, None, ALU.mult), [psr(bank), PSC.r()], [MIX.r(m)])
        postnorm_update(Tt, segs, lambda j: par_idx(l, 1, j))

    def mlp_layer(l, Tt, segs):
        prenorm(Tt, segs, lambda j: par_idx(l, 2, j), lambda j: b_idx(l, 1, j), hb_dest)
        for hf in range(2):
            for s in range(16):
                slot = load_slab(w_up[l][:, (hf * 16 + s) * SCW:(hf * 16 + s + 1) * SCW])
                for mm in range(2):
                    bank = next_bank()
                    lst = [(PS[bank][:, 0:Tt], slot.ap[:, k, mm * 128:(mm + 1) * 128], HB.ap[:, k, 0:Tt], k == 0, k == NK - 1)
                           for k in range(NK)]
                    mm_group(lst, [slot.r(), HB.r()], bank)
                    rl = RL[0]
                    act(lambda e, o=rl.ap[:, 0, 0:Tt], i=PS[bank][:, 0:Tt]: e.activation(out=o, in_=i, func=AF.Relu),
                        [psr(bank)], [rl.r()])
                    dve(lambda e, o=A.ap[:, 2 * s + mm, 0:Tt], i=rl.ap[:, 0, 0:Tt]: e.tensor_tensor(o, i, i, ALU.mult),
                        [rl.r()], [A.r(2 * s + mm)])
            for cb in range(8):
                banks = [next_bank(), next_bank()]
                for kb in range(2):
                    r0 = (hf * 2 + kb) * D
                    slot = load_slab(w_down[l][r0:r0 + D, cb * SCW:(cb + 1) * SCW])
                    for mm in range(2):
                        lst = [(PS[banks[mm]][:, 0:Tt], slot.ap[:, k, mm * 128:(mm + 1) * 128], A.ap[:, kb * 16 + k, 0:Tt],
                                kb == 0 and k == 0, kb == 1 and k == NK - 1) for k in range(NK)]
                        mm_group(lst, [slot.r(), A.r(kb * 16, 16)], banks[mm])
                for mm in range(2):
                    m = 2 * cb + mm
                    if hf == 0:
                        act(lambda e, o=MIX.ap[:, m, 0:Tt], i=PS[banks[mm]][:, 0:Tt]: e.activation(out=o, in_=i, func=AF.Copy),
                            [psr(banks[mm])], [MIX.r(m)])
                    else:
                        dve(lambda e, o=MIX.ap[:, m, 0:Tt], i0=PS[banks[mm]][:, 0:Tt], i1=MIX.ap[:, m, 0:Tt]:
                            e.tensor_tensor(o, i0, i1, ALU.add), [psr(banks[mm]), MIX.r(m)], [MIX.r(m)])
        postnorm_update(Tt, segs, lambda j: par_idx(l, 3, j))

    blk_ctr = [0]

    def attn_block(j2, q0, nq, g, pieces):
        bi = blk_ctr[0]
        blk_ctr[0] += 1
        N = 8 * nq
        half = 4 * nq
        pt = PT[bi % 2]
        rd = RD[bi % 2]
        for p, (kte_ap, kto_ap, kt_r, v_ap, v_r, val_ap, val_r, nk) in enumerate(pieces):
            sb = (bi % 2) * 3 + p
            if sb >= 4:
                sb = sb
            lst = [(PS[sb][0:nk, 0:half].rearrange("p (j q) -> p j q", q=nq), kte_ap, QB.ap[:, 4 * g:4 * g + 4, q0:q0 + nq], True, True),
                   (PS[sb][0:nk, half:N].rearrange("p (j q) -> p j q", q=nq), kto_ap, QB.ap[:, 4 * g:4 * g + 4, q0:q0 + nq], True, True)]
            mm_group(lst, kt_r + [QB.r(4 * g, 4)], sb)
            act(lambda e, o=pt.ap[0:nk, p, 0:N], i=PS[sb][0:nk, 0:N]: e.activation(out=o, in_=i, func=AF.Exp, scale=SCALE),
                [psr(sb)], [pt.r(p)])
        np_ = len(pieces)
        if ASUB < 2:
            return
        lst = [(PS[6][:, 0:N], v_ap, pt.ap[:, p, 0:N], p == 0, p == np_ - 1)
               for p, (_, _, _, v_ap, v_r, _, _, nk) in enumerate(pieces)]
        mm_group(lst, [pt.r()] + [pc[4] for pc in pieces], 6)
        lst = [(PS[7][:, 0:N], val_ap, pt.ap[:, p, 0:N], p == 0, False)
               for p, (_, _, _, _, _, val_ap, val_r, nk) in enumerate(pieces)]
        o3 = PS[7][:, 0:N].rearrange("p (j q) -> p j q", q=nq)
        skh_ap = SKH.ap[:, 0, j2 * 32 + g * 8:j2 * 32 + g * 8 + 8].unsqueeze(2).to_broadcast([128, 8, nq])
        skl_ap = SKL.ap[:, 0, j2 * 32 + g * 8:j2 * 32 + g * 8 + 8].unsqueeze(2).to_broadcast([128, 8, nq])
        lst.append((o3, E0.ap[:, 0, :], skh_ap, False, False))
        lst.append((o3, E0.ap[:, 0, :], skl_ap, False, True))
        mm_group(lst, [pt.r(), VONE.r(), VONE32.r(), VHV.r(), E0.r(), SKH.r(), SKL.r()], 7)
        if ASUB < 3:
            return
        act(lambda e, o=rd.ap[:, 0, 0:N], i=PS[7][:, 0:N]: e.activation(out=o, in_=i, func=AF.Ln), [psr(7)], [rd.r()])
        act(lambda e, o=rd.ap[:, 0, 0:N]: e.activation(out=o, in_=o, func=AF.Exp, scale=-1.0), [rd.r()], [rd.r()])
        if ASUB < 4:
            return
        dve(lambda e, o=OB.ap[0:64, 4 * g:4 * g + 4, q0:q0 + nq],
            i0=PS[6][0:64, 0:half].rearrange("p (j q) -> p j q", q=nq),
            i1=rd.ap[0:64, 0, 0:half].rearrange("p (j q) -> p j q", q=nq): e.tensor_tensor(o, i0, i1, ALU.mult),
            [psr(6), rd.r()], [OB.r(4 * g, 4)])
        dve(lambda e, o=OB.ap[64:128, 4 * g:4 * g + 4, q0:q0 + nq],
            i0=PS[6][64:128, half:N].rearrange("p (j q) -> p j q", q=nq),
            i1=rd.ap[64:128, 0, half:N].rearrange("p (j q) -> p j q", q=nq): e.tensor_tensor(o, i0, i1, ALU.mult),
            [psr(6), rd.r()], [OB.r(4 * g, 4)])

    def attn_layer(l, Tt, segs, blocks):
        j2 = l - 2
        prenorm(Tt, segs, lambda j: par_idx(l, 0, j), lambda j: b_idx(l, 0, j), hb_dest)
        def q_fin_factory(m):
            def fin(t1, t2):
                dve(lambda e, o=QB.ap[:, m, 0:Tt], i0=t1.ap[:, 0, 0:Tt], i1=t2.ap[:, 0, 0:Tt]: e.tensor_tensor(o, i0, i1, ALU.add),
                    [t1.r(), t2.r()], [QB.r(m)])
            return fin

        def gen_q():
            cur_slot = None
            for m in range(NK):
                if m % 2 == 0:
                    cur_slot = load_slab(w_q[j2][:, (m // 2) * SCW:(m // 2 + 1) * SCW])
                mm = m % 2
                lst = [(None, cur_slot.ap[:, k, mm * 128:(mm + 1) * 128], HB.ap[:, k, 0:Tt], k == 0, k == NK - 1) for k in range(NK)]
                yield (lst, [cur_slot.r(), HB.r()], q_fin_factory(m))

        rope_chunks_gen(gen_q(), Tt)
        if Tt == TH:
            dve(lambda e: e.memset(OB.ap[:, :, 0:HALO], 0.0), [], [OB.r()])
        for pt_ in PT:
            dve(lambda e, o=pt_.ap[64:128, :, :]: e.memset(o, 0.0), [], [pt_.r()])
            if Tt == TH:
                dve(lambda e, o=pt_.ap[32:64, 2, :]: e.memset(o, 0.0), [], [pt_.r()])
        for (q0, nq, g, pieces) in blocks:
            if ASUB >= 1:
                attn_block(j2, q0, nq, g, pieces)
        if ASUB < 5:
            return
        for s in range(8):
            slot = load_slab(w_o[j2][:, s * SCW:(s + 1) * SCW])
            for mm in range(2):
                m = 2 * s + mm
                bank = next_bank()
                lst = [(PS[bank][:, 0:Tt], slot.ap[:, k, mm * 128:(mm + 1) * 128], OB.ap[:, k, 0:Tt], k == 0, k == NK - 1)
                       for k in range(NK)]
                mm_group(lst, [slot.r(), OB.r()], bank)
                act(lambda e, o=MIX.ap[:, m, 0:Tt], i=PS[bank][:, 0:Tt]: e.activation(out=o, in_=i, func=AF.Copy),
                    [psr(bank)], [MIX.r(m)])
        postnorm_update(Tt, segs, lambda j: par_idx(l, 1, j))

    def rope_chunks_gen(gen, Tt):
        pending = None

        def finish(p):
            qf, t1, t2, fin, qh, ql = p
            act(lambda e, o=qh.ap[:, 0, 0:Tt], i=qf.ap[:, 0, 0:Tt]: e.activation(out=o, in_=i, func=AF.Copy), [qf.r()], [qh.r()])
            dve(lambda e, o=ql.ap[:, 0, 0:Tt], i0=qf.ap[:, 0, 0:Tt], i1=qh.ap[:, 0, 0:Tt]: e.tensor_tensor(o, i0, i1, ALU.subtract),
                [qf.r(), qh.r()], [ql.r()])
            mm_group([(PS[5][:, 0:Tt], PMB.ap[:, 0, :], qh.ap[:, 0, 0:Tt], True, False),
                      (PS[5][:, 0:Tt], PMB.ap[:, 0, :], ql.ap[:, 0, 0:Tt], False, True)], [qh.r(), ql.r(), PMB.r()], 5)
            dve(lambda e, o=t1.ap[:, 0, 0:Tt], i0=qf.ap[:, 0, 0:Tt], i1=COS.ap[:, 0, 0:Tt]: e.tensor_tensor(o, i0, i1, ALU.mult),
                [qf.r(), COS.r()], [t1.r()])
            dve(lambda e, o=t2.ap[:, 0, 0:Tt], i0=PS[5][:, 0:Tt], i1=SIN.ap[:, 0, 0:Tt]: e.tensor_tensor(o, i0, i1, ALU.mult),
                [psr(5), SIN.r()], [t2.r()])
            fin(t1, t2)

        for idx, (lst, reads, fin) in enumerate(gen):
            bank = next_bank()
            lst = [(PS[bank][:, 0:Tt], l_, r_, st, sp) for (_, l_, r_, st, sp) in lst]
            mm_group(lst, reads, bank)
            qf, t1, t2 = QF[idx % 2], T1[idx % 2], T2[idx % 2]
            act(lambda e, o=qf.ap[:, 0, 0:Tt], i=PS[bank][:, 0:Tt]: e.activation(out=o, in_=i, func=AF.Copy),
                [psr(bank)], [qf.r()])
            if pending is not None:
                finish(pending)
            pending = (qf, t1, t2, fin, QH[idx % 2], QL[idx % 2])
        if pending is not None:
            finish(pending)

    def kv_phase(Tt, segs, carry, vblocks, kouts, vouts):
        if carry is not None:
            ksrc, vsrc, scaled = carry
            for KX in (KTE, KTO):
                act(lambda e, o=KX.ap[:, :, 0:128], i=KX.ap[:, :, ksrc:ksrc + 128]: e.activation(out=o, in_=i, func=AF.Copy),
                    [KX.r()], [KX.r()])
            for d_, s_ in enumerate(vsrc):
                if scaled:
                    dve(lambda e, o=VB.ap[:, d_, :, :], i=VB.ap[:, s_, :, :]: e.tensor_scalar(o, i, HV.ap[:, 0, :], None, ALU.mult),
                        [VB.r(s_), HV.r()], [VB.r(d_)])
                else:
                    dve(lambda e, o=VB.ap[:, d_, :, :], i=VB.ap[:, s_, :, :]: e.tensor_copy(o, i), [VB.r(s_)], [VB.r(d_)])
        prenorm(Tt, segs, lambda j: ak_idx(j), lambda j: bk_idx(j), hb_dest)

        def k_fin_factory(g):
            def fin(t1, t2):
                dve(lambda e, o=t1.ap[:, 0, 0:Tt], i0=t1.ap[:, 0, 0:Tt], i1=t2.ap[:, 0, 0:Tt]: e.tensor_tensor(o, i0, i1, ALU.add),
                    [t1.r(), t2.r()], [t1.r()])
                act(lambda e, o=KTE.ap[0:64, g, 128:128 + Tt], i=t1.ap[0:64, 0, 0:Tt]: e.activation(out=o, in_=i, func=AF.Copy),
                    [t1.r()], [KTE.r(g)])
                act(lambda e, o=KTO.ap[64:128, g, 128:128 + Tt], i=t1.ap[64:128, 0, 0:Tt]: e.activation(out=o, in_=i, func=AF.Copy),
                    [t1.r()], [KTO.r(g)])
                for (src0, n, dst0) in kouts:
                    dve(lambda e, o=KOUT.ap[:, g, dst0:dst0 + n], i=t1.ap[0:64, 0, src0:src0 + n]: e.tensor_copy(o, i),
                        [t1.r()], [KOUT.r(g)])
            return fin

        def gen_k():
            cur_slot = None
            for g in range(4):
                if g % 2 == 0:
                    cur_slot = load_slab(wkd[:, (g // 2) * SCW:(g // 2 + 1) * SCW])
                mm = g % 2
                lst = [(None, cur_slot.ap[:, k, mm * 128:(mm + 1) * 128], HB.ap[:, k, 0:Tt], k == 0, k == NK - 1) for k in range(NK)]
                yield (lst, [cur_slot.r(), HB.r()], k_fin_factory(g))

        if SUB >= 1:
            rope_chunks_gen(gen_k(), Tt)
        if SUB < 2:
            return
        slot = load_slab(wv[:, :])
        for (c0, n, blk) in vblocks:
            bank = next_bank()
            lst = [(PS[bank][0:n, 0:256], HB.ap[:, k, c0:c0 + n], slot.ap[:, k, 0:256], k == 0, k == NK - 1) for k in range(NK)]
            mm_group(lst, [slot.r(), HB.r()], bank)
            psv = PS[bank][0:n, 0:256].rearrange("p (g d) -> p g d", d=64)
            if SUB < 3:
                continue
            act(lambda e, o=VB.ap[0:n, blk, :, 0:64], i=psv: e.activation(out=o, in_=i, func=AF.Copy), [psr(bank)], [VB.r(blk)])
            act(lambda e, o=VB.ap[0:n, blk, :, 64:128], i=psv: e.activation(out=o, in_=i, func=AF.Copy), [psr(bank)], [VB.r(blk)])
            for (sc0, oblk) in vouts:
                if sc0 == c0 and SUB >= 4:
                    act(lambda e, o=VOUT.ap[0:n, oblk, :], i=PS[bank][0:n, 0:256]: e.activation(out=o, in_=i, func=AF.Copy),
                        [psr(bank)], [VOUT.r(oblk)])

    xT_v = xT.rearrange("(k p) n -> p k n", p=128)
    yT_v = yT.rearrange("(k p) n -> p k n", p=128)

    def run_tile(ti):
        pg.phase(f"t{ti}")
        is_hs = ti == 0
        if is_hs:
            Tt, col0 = TH, 0
            segs = [dict(c0=0, n=HALO, j=0, pfx="PF"), dict(c0=HALO, n=NSS, j=1, pfx=0), dict(c0=HALO + NSS, n=NSS, j=2, pfx=1)]
            vblocks = [(0, 64, 2), (64, 64, 3), (128, 64, 4), (192, 32, 5), (224, 32, 6)]
            carry = None
            kouts = [(HALO, 2 * NSS, 128)]
            vouts = [(192, 2), (224, 3)]
        else:
            Tt, col0 = TM, TH + (ti - 1) * TM
            segs = [dict(c0=0, n=TM, j=0, pfx="PF")]
            vblocks = [(64 * i, 64, 2 + i) for i in range(8)]
            carry = (192, [3, 4], True) if ti == 1 else (512, [8, 9], False)
            kouts = [(384, 128, 0)] if ti == NMAIN else []
            vouts = [(384, 0), (448, 1)] if ti == NMAIN else []
        pg.op("sync", lambda e: e.dma_start(out=X.ap[:, :, 0:Tt], in_=xT_v[:, :, col0:col0 + Tt]), reads=[], writes=[X.r()], dsem=s_x)
        pg.op("sync", lambda e: e.dma_start(out=COS.ap[:, 0, 0:Tt], in_=cosT[:, col0:col0 + Tt]), reads=[], writes=[COS.r()], dsem=s_tab)
        pg.op("sync", lambda e: e.dma_start(out=SIN.ap[:, 0, 0:Tt], in_=sinT[:, col0:col0 + Tt]), reads=[], writes=[SIN.r()], dsem=s_tab)
        tv = pg.cnt[s_tab]
        for b in (COS, SIN):
            for seg in pg.maps["sb"].cover(b.off, b.off + b.nbytes):
                seg[2] = (s_tab, tv)
        blocks = []
        if is_hs:
            for s in range(2):
                q0 = HALO + NSS * s
                for g in range(4):
                    pieces = []
                    for pc in range(2):
                        pieces.append((CKTE.ap[:, s, g, pc * 64:(pc + 1) * 64], CKTO.ap[:, s, g, pc * 64:(pc + 1) * 64],
                                       [CKTE.r(), CKTO.r()], CVB.ap[:, s, pc, g, :], CVB.r(), VONE.ap[:, 0, :], VONE.r(), 64))
                    pieces.append((KTE.ap[:, g, 128 + q0:128 + q0 + NSS], KTO.ap[:, g, 128 + q0:128 + q0 + NSS],
                                   [KTE.r(g), KTO.r(g)], VB.ap[:, 5 + s, g, :], VB.r(5 + s), VONE32.ap[:, 0, :], VONE32.r(), NSS))
                    blocks.append((q0, NSS, g, pieces))
        else:
            for c in range(8):
                for g in range(4):
                    pieces = []
                    for pc in range(3):
                        blk = c + pc
                        val = VHV if (ti == 1 and blk < 2) else VONE
                        pieces.append((KTE.ap[:, g, blk * 64:(blk + 1) * 64], KTO.ap[:, g, blk * 64:(blk + 1) * 64],
                                       [KTE.r(g), KTO.r(g)], VB.ap[:, blk, g, :], VB.r(blk), val.ap[:, 0, :], val.r(), 64))
                    blocks.append((c * 64, 64, g, pieces))
        for l in range(DEPTH):
            if l < 2:
                if gate(10 * ti + 3 * l + 1):
                    pool_layer(l, Tt, segs, first_main=(ti == 1), is_hs=is_hs)
            else:
                if gate(10 * ti + 3 * l + 1):
                    attn_layer(l, Tt, segs, blocks)
            if gate(10 * ti + 3 * l + 2):
                mlp_layer(l, Tt, segs)
            if l == 1:
                if gate(10 * ti + 3 * l + 3):
                    kv_phase(Tt, segs, carry, vblocks, kouts, vouts)
        if is_hs:
            pg.op("sync", lambda e: e.dma_start(out=yT_v[:, :, 0:2 * NSS], in_=X.ap[:, :, HALO:TH]), reads=[X.r()], writes=[], dsem=s_y)
        else:
            oc = 2 * NSS + (ti - 1) * TM
            pg.op("sync", lambda e: e.dma_start(out=yT_v[:, :, oc:oc + TM], in_=X.ap[:, :, 0:TM]), reads=[X.r()], writes=[], dsem=s_y)

    for ti in range(NMAIN + 1):
        run_tile(ti)

    for l in range(2):
        pv = poolT[l].rearrange("(k p) r -> p k r", p=128)
        pg.op("sync", lambda e, o=pv[:, :, 0:15], i=PF.ap[:, l, :, :]: e.dma_start(out=o, in_=i), reads=[PF.r(l)], writes=[], dsem=s_out)
        for s in range(2):
            pg.op("sync", lambda e, o=pv[:, :, 15 + 15 * s:30 + 15 * s], i=SPF.ap[:, l, :, s, :]: e.dma_start(out=o, in_=i),
                  reads=[SPF.r(l)], writes=[], dsem=s_out)
    pg.op("sync", lambda e: e.dma_start(out=kout.rearrange("g d n -> d g n"), in_=KOUT.ap), reads=[KOUT.r()], writes=[], dsem=s_out)
    pg.op("sync", lambda e: e.dma_start(out=vout.rearrange("b t f -> t b f"), in_=VOUT.ap), reads=[VOUT.r()], writes=[], dsem=s_out)
    finals = [(s_y, pg.cnt[s_y]), (s_out, pg.cnt[s_out])]

    with nc.Block() as block:
        @block.tensor
        def _(e):
            pg.emit("pe", e)

        @block.scalar
        def _(e):
            pg.emit("act", e)

        @block.vector
        def _(e):
            pg.emit("dve", e)

        @block.gpsimd
        def _(e):
            pg.emit("pool", e)

        @block.sync
        def _(e):
            pg.emit("sync", e, final_waits=finals)
    return nc


def _fm(v):
    v = np.asarray(v, np.float32)
    lead = v.shape[:-1]
    a = v.reshape(lead + (NK, 128))
    a = np.moveaxis(a, -1, 0)
    return np.ascontiguousarray(a)


_NC_CACHE = {}


def kernel(**inputs):
    in_maps = _prepare(**inputs)
    if "nc" not in _NC_CACHE:
        _NC_CACHE["nc"] = build_program()
    nc = _NC_CACHE["nc"]
    res = run_bass_kernel_spmd(nc, in_maps, core_ids=list(range(8)))
    return _assemble(res.results)


def _prepare(x_prompt, x_sample, c_prompt, c_sample, state_pool, cache_k, cache_v,
             w_mod, b_mod, g_norm, w_pool, pool_scale, w_kv_mod, b_kv_mod, g_kv, w_kv,
             w_q, sinks, w_o, w_up, w_down):
    f32 = np.float32
    x_prompt = np.asarray(x_prompt, f32)
    x_sample = np.asarray(x_sample, f32)
    c_prompt = np.asarray(c_prompt, f32)
    c_sample = np.asarray(c_sample, f32)
    state_pool = np.asarray(state_pool, f32)
    cache_k = np.asarray(cache_k, f32)
    cache_v = np.asarray(cache_v, f32)
    w_mod = np.ascontiguousarray(np.asarray(w_mod, f32))
    w_kv_mod = np.ascontiguousarray(np.asarray(w_kv_mod, f32))
    w_kv = np.asarray(w_kv, f32)
    w_q = np.ascontiguousarray(np.asarray(w_q, f32))
    w_o = np.ascontiguousarray(np.asarray(w_o, f32))
    w_up = np.ascontiguousarray(np.asarray(w_up, f32))
    w_down = np.ascontiguousarray(np.asarray(w_down, f32))
    sinks = np.asarray(sinks, f32)


    gT = _fm(np.asarray(g_norm, f32).reshape(16, D))
    psT = _fm(np.asarray(pool_scale, f32))
    bmodT = np.ascontiguousarray(np.moveaxis(np.asarray(b_mod, f32).reshape(4, 96, 128), -1, 0))
    bkvT = np.ascontiguousarray(np.asarray(b_kv_mod, f32).reshape(32, 128).T)
    gkvT = np.ascontiguousarray(np.asarray(g_kv, f32).reshape(NK, 128).T)
    w_pool2 = np.ascontiguousarray(np.asarray(w_pool, f32).reshape(2, D, 512))
    wk = w_kv[:, :256].reshape(D, 4, 1, 64)
    wkd = np.ascontiguousarray(np.broadcast_to(wk, (D, 4, 2, 64)).reshape(D, 512))
    wv = np.ascontiguousarray(w_kv[:, 256:])
    sk = np.ascontiguousarray(sinks.reshape(2, 4, 4, 2).transpose(0, 1, 3, 2).reshape(1, 64))
    pmat = np.zeros((128, 128), f32)
    for m in range(128):
        d = m % 64
        partner = m + 32 if d < 32 else m - 32
        pmat[partner, m] = 1.0
    inv = (np.float32(10000.0) ** (-(np.arange(32, dtype=f32) / np.float32(32)))).astype(f32)

    in_maps = []
    for c in range(8):
        b, half = c // 2, c % 2
        m0 = half * 2048
        xT = np.zeros((D, NCOLS), f32)
        pos = np.zeros((NCOLS,), f32)
        if half == 1:
            xT[:, 0:HALO] = x_prompt[b, m0 - HALO:m0].T
            pos[0:HALO] = np.arange(m0 - HALO, m0)
        for s in range(2):
            xT[:, HALO + NSS * s:HALO + NSS * (s + 1)] = x_sample[2 * c + s].T
            pos[HALO + NSS * s:HALO + NSS * (s + 1)] = 4096 + np.arange(NSS)
        xT[:, TH:] = x_prompt[b, m0:m0 + NMAIN * TM].T
        pos[TH:] = np.arange(m0, m0 + NMAIN * TM)
        ang = pos.astype(f32)[None, :] * inv[:, None]
        cs = np.cos(ang).astype(f32)
        sn = np.sin(ang).astype(f32)
        cos64 = np.concatenate([cs, cs], 0)
        sin64 = np.concatenate([-sn, sn], 0)
        cosT = np.ascontiguousarray(np.concatenate([cos64, cos64], 0))
        sinT = np.ascontiguousarray(np.concatenate([sin64, sin64], 0))
        cvecs = np.stack([c_prompt[b], c_sample[2 * c], c_sample[2 * c + 1]], 0)
        cTt = np.ascontiguousarray(cvecs.reshape(3, NK, 128).transpose(2, 1, 0))
        invc = np.zeros((128, 4, 16), f32)
        for g in range(4):
            w = 2 << g
            cnt = np.minimum(w, m0 + np.arange(16) + 1).astype(f32)
            invc[:, g, :] = (np.float32(1.0) / cnt)[None, :]
        hvv = np.full((128, 1), float(half), f32)
        sp = state_pool[:, 2 * c:2 * c + 2]
        spT = np.ascontiguousarray(sp.reshape(2, 2, 15, NK, 128).transpose(4, 0, 3, 1, 2))
        ck = cache_k[2 * c:2 * c + 2]
        ckt = ck.transpose(3, 0, 2, 1)
        ckT = np.ascontiguousarray(np.concatenate([ckt, ckt], 0))
        cv = cache_v[2 * c:2 * c + 2].reshape(2, 2, 64, 4, 64)
        cvt = cv.transpose(2, 0, 1, 3, 4)
        cvT = np.ascontiguousarray(np.concatenate([cvt, cvt], -1))
        in_maps.append(dict(
            xT=xT, cT=cTt, gT=gT, psT=psT, bmodT=bmodT, bkvT=bkvT, gkvT=gkvT, cosT=cosT, sinT=sinT, invc=invc, hv=hvv,
            pm=pmat, sk=sk, spT=spT, ckT=ckT, cvT=cvT, w_mod=w_mod, w_kv_mod=w_kv_mod, w_pool=w_pool2, wkd=wkd, wv=wv,
            w_q=w_q, w_o=w_o, w_up=w_up, w_down=w_down))
    return in_maps


def _assemble(outs):
    f32 = np.float32

    y_prompt = np.zeros((4, 4096, D), f32)
    y_sample = np.zeros((16, 32, D), f32)
    pool_p = np.zeros((2, 4, 15, D), f32)
    pool_s = np.zeros((2, 16, 15, D), f32)
    k_p = np.zeros((4, 128, 4, 64), f32)
    v_p = np.zeros((4, 128, 4, 64), f32)
    k_s = np.zeros((16, 32, 4, 64), f32)
    v_s = np.zeros((16, 32, 4, 64), f32)
    for c in range(8):
        b, half = c // 2, c % 2
        o = outs[c]
        yT = np.asarray(o["yT"])
        y_prompt[b, half * 2048:(half + 1) * 2048] = yT[:, 2 * NSS:].T
        for s in range(2):
            y_sample[2 * c + s] = yT[:, NSS * s:NSS * (s + 1)].T
        pT = np.asarray(o["poolT"])
        ko = np.asarray(o["kout"])
        vo = np.asarray(o["vout"])
        for l in range(2):
            if half == 1:
                pool_p[l, b] = pT[l, :, 0:15].T
            for s in range(2):
                pool_s[l, 2 * c + s] = pT[l, :, 15 + 15 * s:30 + 15 * s].T
        if half == 1:
            k_p[b] = ko[:, :, 0:128].transpose(2, 0, 1)
            v_p[b] = np.concatenate([vo[0], vo[1]], 0).reshape(128, 4, 64)
        for s in range(2):
            k_s[2 * c + s] = ko[:, :, 128 + NSS * s:128 + NSS * (s + 1)].transpose(2, 0, 1)
            v_s[2 * c + s] = vo[2 + s][0:NSS].reshape(NSS, 4, 64)
    return (y_prompt, y_sample, pool_p, pool_s, k_p, v_p, k_s, v_s)
```
